# Optimizing a Trainium2 kernel written in Bass

```python
import jax, jax.numpy as jnp
from jax import lax
import numpy as np

D_MODEL = 1024
BATCH = 8
SEQ = 2048
DEPTH = 1
DEC_BATCH = 128
DEC_SEQ = 8
PAST_LEN = 16384
PAGE_SIZE = 128

D_PLE = 256
CONF_WIDTH = D_MODEL // 2
CONF_K = 31
DN_HEADS = 4
DN_HEAD_DIM = 128
DN_WIDTH = DN_HEADS * DN_HEAD_DIM
DN_CONV_K = 4
MIX_WIDTH = CONF_WIDTH + DN_WIDTH
IN_COLS = 2 * CONF_WIDTH + 4 * DN_WIDTH + 2 * DN_HEADS
D_FF = 2816
FFN_CONV_K = 3
CHUNK = 64
EPS = 1e-6

kernel_name = 'hybrid_conformer_gdn_step'


def _rmsnorm(x, g):
    xf = x.astype(jnp.float32)
    y = xf * lax.rsqrt(jnp.mean(jnp.square(xf), axis=-1, keepdims=True) + EPS) * g.astype(jnp.float32)
    return y.astype(x.dtype)


def _l2norm(x):
    return x * lax.rsqrt(jnp.sum(jnp.square(x), axis=-1, keepdims=True) + EPS)


def _causal_dwconv(x, buf, w, b=None):
    K, C = w.shape
    xp = jnp.concatenate([buf.astype(x.dtype), x], axis=1)
    y = lax.conv_general_dilated(xp, w[:, None, :].astype(x.dtype), window_strides=(1,), padding='VALID',
                                 dimension_numbers=('NWC', 'WIO', 'NWC'), feature_group_count=C)
    if b is not None:
        y = y + b.astype(y.dtype)
    return y, xp[:, xp.shape[1] - (K - 1):]


def _split_cols(u):
    outs, o = [], 0
    for wdt in (CONF_WIDTH, CONF_WIDTH, 3 * DN_WIDTH, DN_WIDTH, DN_HEADS, DN_HEADS):
        outs.append(u[..., o:o + wdt])
        o += wdt
    return outs


def _conformer_conv(val, gate, buf, w, b, ln_g, ln_b):
    glu = val * jax.nn.sigmoid(gate)
    c, new_buf = _causal_dwconv(glu, buf, w, b)
    cf = c.astype(jnp.float32)
    mu = jnp.mean(cf, axis=-1, keepdims=True)
    var = jnp.mean(jnp.square(cf - mu), axis=-1, keepdims=True)
    n = (cf - mu) * lax.rsqrt(var + EPS) * ln_g.astype(jnp.float32) + ln_b.astype(jnp.float32)
    return jax.nn.silu(n).astype(val.dtype), new_buf


def _gated_delta_chunked(q, k, v, g, beta, S0):
    Bn, L, H, Dk = q.shape
    Dv = v.shape[-1]
    pad = (-L) % CHUNK
    if pad:
        pw = ((0, 0), (0, pad), (0, 0), (0, 0))
        q, k, v = jnp.pad(q, pw), jnp.pad(k, pw), jnp.pad(v, pw)
        g, beta = jnp.pad(g, pw[:3]), jnp.pad(beta, pw[:3])
    N = (L + pad) // CHUNK

    def blk(t):
        t = t.reshape((Bn, N, CHUNK) + t.shape[2:])
        return jnp.moveaxis(jnp.moveaxis(t, 1, 0), 2, 3)

    qc = blk(q) * (Dk ** -0.5)
    kc, vc, bc = blk(k), blk(v), blk(beta)
    gc = jnp.cumsum(blk(g), axis=-1)
    idx = jnp.arange(CHUNK)
    causal = idx[:, None] >= idx[None, :]
    strict = idx[:, None] > idx[None, :]
    decay = jnp.exp(jnp.where(causal, gc[..., :, None] - gc[..., None, :], -jnp.inf))
    kb = kc * bc[..., None]
    a_low = jnp.where(strict, jnp.einsum('nbhid,nbhjd->nbhij', kb, kc) * decay, 0.0)
    rhs = jnp.concatenate([vc * bc[..., None], kb * jnp.exp(gc)[..., None]], axis=-1)
    sol = lax.linalg.triangular_solve(a_low, rhs, left_side=True, lower=True, unit_diagonal=True)
    u, w = sol[..., :Dv], sol[..., Dv:]
    qk = jnp.where(causal, jnp.einsum('nbhid,nbhjd->nbhij', qc, kc) * decay, 0.0)
    q_dec = qc * jnp.exp(gc)[..., None]
    k_dec = kc * jnp.exp(gc[..., -1:] - gc)[..., None]
    g_tot = jnp.exp(gc[..., -1])[..., None, None]

    def step(S, xs):
        u_i, w_i, qk_i, qd_i, kd_i, gt_i = xs
        v_new = u_i - jnp.einsum('bhck,bhkv->bhcv', w_i, S)
        o_i = jnp.einsum('bhck,bhkv->bhcv', qd_i, S) + jnp.einsum('bhij,bhjv->bhiv', qk_i, v_new)
        S = S * gt_i + jnp.einsum('bhck,bhcv->bhkv', kd_i, v_new)
        return S, o_i

    S_fin, o = lax.scan(step, S0, (u, w, qk, q_dec, k_dec, g_tot))
    o = jnp.moveaxis(jnp.moveaxis(o, 3, 2), 0, 1).reshape(Bn, N * CHUNK, H, Dv)[:, :L]
    return o, S_fin


def _layer(x, p, conf_buf, dn_buf, S, ffn_buf, norm_mix_g, w_in, conf_dw_w, conf_dw_b, conf_ln_g, conf_ln_b,
           dn_conv_w, dn_a_log, dn_dt_bias, dn_norm_g, w_out, norm_ffn_g, w_up, ffn_conv_w, ffn_conv_b,
           w_down, w_ple, w_ple_gate):
    Bn, L, _ = x.shape
    h = _rmsnorm(x, norm_mix_g)
    u = h @ w_in
    c_val, c_gate, qkv, z, b_in, a_in = _split_cols(u)
    conf_out, conf_buf_new = _conformer_conv(c_val, c_gate, conf_buf, conf_dw_w, conf_dw_b, conf_ln_g, conf_ln_b)
    qkv_c, dn_buf_new = _causal_dwconv(qkv, dn_buf, dn_conv_w)
    qkv_c = jax.nn.silu(qkv_c.astype(jnp.float32))
    q, k, v = [t.reshape(Bn, L, DN_HEADS, DN_HEAD_DIM) for t in jnp.split(qkv_c, 3, axis=-1)]
    q, k = _l2norm(q), _l2norm(k)
    beta = jax.nn.sigmoid(b_in.astype(jnp.float32))
    g = -jnp.exp(dn_a_log.astype(jnp.float32)) * jax.nn.softplus(a_in.astype(jnp.float32) + dn_dt_bias.astype(jnp.float32))
    o, S_new = _gated_delta_chunked(q, k, v, g, beta, S.astype(jnp.float32))
    o = _rmsnorm(o, dn_norm_g) * jax.nn.silu(z.astype(jnp.float32).reshape(Bn, L, DN_HEADS, DN_HEAD_DIM))
    o = o.reshape(Bn, L, DN_WIDTH).astype(x.dtype)
    x = x + jnp.concatenate([conf_out, o], axis=-1) @ w_out
    up, ffn_buf_new = _causal_dwconv(_rmsnorm(x, norm_ffn_g) @ w_up, ffn_buf, ffn_conv_w, ffn_conv_b)
    gate, val = jnp.split(up, 2, axis=-1)
    x = x + (jax.nn.silu(gate) * val) @ w_down
    x = x + (p @ w_ple) * jax.nn.sigmoid(x @ w_ple_gate)
    return x, conf_buf_new, dn_buf_new, S_new.astype(S.dtype), ffn_buf_new


def _trunk(x, p, conf_buf, dn_buf, S, ffn_buf, layer_w, norm_final_g):
    outs = []
    for i in range(DEPTH):
        x, cb, db, s, fb = _layer(x, p[i], conf_buf[i], dn_buf[i], S[i], ffn_buf[i], *[wt[i] for wt in layer_w])
        outs.append((cb, db, s, fb))
    new = [jnp.stack([o[j] for o in outs]) for j in range(4)]
    return _rmsnorm(x, norm_final_g), new[0], new[1], new[2], new[3]


def setup_inputs(seed: int = 0) -> dict:
    key = jax.random.key(seed)
    ks = jax.random.split(key, 32)
    nrm = lambda k, shape, s=1.0: jax.random.normal(k, shape, jnp.float32) * s
    gain = lambda k, shape: 1.0 + 0.02 * jax.random.normal(k, shape, jnp.float32)
    dt = jnp.exp(jax.random.uniform(ks[20], (DEPTH, DN_HEADS), jnp.float32, np.log(1e-3), np.log(1e-1)))
    return {
        'x_prompt': nrm(ks[0], (BATCH, SEQ, D_MODEL)),
        'x_sample': nrm(ks[1], (DEC_BATCH, DEC_SEQ, D_MODEL)),
        'p_prompt': nrm(ks[2], (DEPTH, BATCH, SEQ, D_PLE)),
        'p_sample': nrm(ks[3], (DEPTH, DEC_BATCH, DEC_SEQ, D_PLE)),
        'state_conf_buf': nrm(ks[4], (DEPTH, DEC_BATCH, CONF_K - 1, CONF_WIDTH), 0.5),
        'state_dn_conv_buf': nrm(ks[5], (DEPTH, DEC_BATCH, DN_CONV_K - 1, 3 * DN_WIDTH)),
        'state_dn_S': nrm(ks[6], (DEPTH, DEC_BATCH, DN_HEADS, DN_HEAD_DIM, DN_HEAD_DIM)),
        'state_ffn_buf': nrm(ks[7], (DEPTH, DEC_BATCH, FFN_CONV_K - 1, 2 * D_FF)),
        'norm_mix_g': gain(ks[8], (DEPTH, D_MODEL)),
        'w_in': nrm(ks[9], (DEPTH, D_MODEL, IN_COLS), D_MODEL ** -0.5),
        'conf_dw_w': nrm(ks[10], (DEPTH, CONF_K, CONF_WIDTH), CONF_K ** -0.5),
        'conf_dw_b': nrm(ks[11], (DEPTH, CONF_WIDTH), 0.01),
        'conf_ln_g': gain(ks[12], (DEPTH, CONF_WIDTH)),
        'conf_ln_b': nrm(ks[13], (DEPTH, CONF_WIDTH), 0.02),
        'dn_conv_w': nrm(ks[14], (DEPTH, DN_CONV_K, 3 * DN_WIDTH), DN_CONV_K ** -0.5),
        'dn_a_log': jnp.log(jax.random.uniform(ks[15], (DEPTH, DN_HEADS), jnp.float32, 1.0, 16.0)),
        'dn_dt_bias': dt + jnp.log(-jnp.expm1(-dt)),
        'dn_norm_g': gain(ks[16], (DEPTH, DN_HEAD_DIM)),
        'w_out': nrm(ks[17], (DEPTH, MIX_WIDTH, D_MODEL), MIX_WIDTH ** -0.5),
        'norm_ffn_g': gain(ks[18], (DEPTH, D_MODEL)),
        'w_up': nrm(ks[19], (DEPTH, D_MODEL, 2 * D_FF), D_MODEL ** -0.5),
        'ffn_conv_w': nrm(ks[21], (DEPTH, FFN_CONV_K, 2 * D_FF), FFN_CONV_K ** -0.5),
        'ffn_conv_b': nrm(ks[22], (DEPTH, 2 * D_FF), 0.01),
        'w_down': nrm(ks[23], (DEPTH, D_FF, D_MODEL), D_FF ** -0.5),
        'w_ple': nrm(ks[24], (DEPTH, D_PLE, D_MODEL), D_PLE ** -0.5),
        'w_ple_gate': nrm(ks[25], (DEPTH, D_MODEL, D_MODEL), D_MODEL ** -0.5),
        'norm_final_g': gain(ks[26], (D_MODEL,)),
    }


def reference(x_prompt, x_sample, p_prompt, p_sample, state_conf_buf, state_dn_conv_buf, state_dn_S, state_ffn_buf,
              norm_mix_g, w_in, conf_dw_w, conf_dw_b, conf_ln_g, conf_ln_b, dn_conv_w, dn_a_log, dn_dt_bias,
              dn_norm_g, w_out, norm_ffn_g, w_up, ffn_conv_w, ffn_conv_b, w_down, w_ple, w_ple_gate, norm_final_g):
    layer_w = (norm_mix_g, w_in, conf_dw_w, conf_dw_b, conf_ln_g, conf_ln_b, dn_conv_w, dn_a_log, dn_dt_bias,
               dn_norm_g, w_out, norm_ffn_g, w_up, ffn_conv_w, ffn_conv_b, w_down, w_ple, w_ple_gate)
    Bp, dt = x_prompt.shape[0], x_prompt.dtype
    z_conf = jnp.zeros((DEPTH, Bp, CONF_K - 1, CONF_WIDTH), dt)
    z_dn = jnp.zeros((DEPTH, Bp, DN_CONV_K - 1, 3 * DN_WIDTH), dt)
    z_S = jnp.zeros((DEPTH, Bp, DN_HEADS, DN_HEAD_DIM, DN_HEAD_DIM), dt)
    z_ffn = jnp.zeros((DEPTH, Bp, FFN_CONV_K - 1, 2 * D_FF), dt)
    y_prompt, pc, pd, ps, pf = _trunk(x_prompt, p_prompt, z_conf, z_dn, z_S, z_ffn, layer_w, norm_final_g)
    y_sample, sc, sd, ss, sf = _trunk(x_sample, p_sample, state_conf_buf, state_dn_conv_buf, state_dn_S,
                                      state_ffn_buf, layer_w, norm_final_g)
    return (y_prompt, y_sample, pc, pd, ps, pf, sc, sd, ss, sf)
```

```python
import contextlib
import numpy as np
import concourse.bass as bass
import concourse.mybir as mybir
from concourse.bass_utils import run_bass_kernel_spmd

F32 = mybir.dt.float32
BF16 = mybir.dt.bfloat16
AF = mybir.ActivationFunctionType
ALU = mybir.AluOpType
AX = mybir.AxisListType

D = 1024
SEQ = 2048
NCORE = 8
DEC_PER = 16
DEC_SEQ = 8
CW = 512
CK = 31
DNW = 512
DK = 4
INC = 3080
DFF = 2816
FK = 3
EPS = 1e-6
NEG = -30000.0


class Prog:
    ENG = ("pe", "act", "dve", "pool", "sp")

    def __init__(self, nc):
        self.nc = nc
        self.ops = []
        self.last_w = {}
        self.readers = {}
        import os
        self.limit = int(os.environ.get("MK_LIMIT", "100000000"))

    def op(self, eng, fn, reads=(), writes=(), dma=None):
        if len(self.ops) >= self.limit:
            return -1
        i = len(self.ops)
        writes = list(writes) + [r for r in reads if isinstance(r, tuple) and r[0] == "pb" and r not in writes]
        deps = set()
        for r in reads:
            if r in self.last_w:
                deps.add(self.last_w[r])
        for w in writes:
            if w in self.last_w:
                deps.add(self.last_w[w])
            for rd in self.readers.get(w, ()):
                deps.add(rd)
        for r in reads:
            lst = self.readers.setdefault(r, [])
            if dma is None:
                lst[:] = [j for j in lst if self.ops[j]["dma"] is not None or self.ops[j]["eng"] != eng]
            lst.append(i)
        for w in writes:
            self.last_w[w] = i
            self.readers[w] = []
        deps.discard(i)
        if dma is not None:
            deps = {d for d in deps if self.ops[d]["dma"] != dma}
        best = {}
        keep = set()
        for d in deps:
            od = self.ops[d]
            if od["dma"] is not None:
                keep.add(d)
            else:
                if od["eng"] not in best or best[od["eng"]] < d:
                    best[od["eng"]] = d
        keep.update(best.values())
        import sys as _sys
        f = _sys._getframe(1)
        while f.f_code.co_name not in ("do_block", "build_program", "norm_T", "load_cols", "wload", "tr_out") and f.f_back is not None:
            f = f.f_back
        self.ops.append(dict(eng=eng, fn=fn, deps=keep, dma=dma, line=f.f_lineno))
        return i

    def barrier(self):
        last = {}
        for i, o in enumerate(self.ops):
            k = ("dma", o["dma"]) if o["dma"] is not None else ("eng", o["eng"])
            last[k] = i
        deps = set(last.values())
        if len(self.ops) >= self.limit:
            return
        for e in self.ENG:
            self.ops.append(dict(eng=e, fn=lambda E: E.nop(), deps=set(deps), dma=None))

    def emit(self):
        nc = self.nc
        ops = self.ops
        engs = dict(pe=nc.tensor, act=nc.scalar, dve=nc.vector, pool=nc.gpsimd, sp=nc.sync)
        need = [False] * len(ops)
        for o in ops:
            for d in o["deps"]:
                od = ops[d]
                if od["dma"] is None and (od["eng"] != o["eng"] or o["dma"] is not None or o["eng"] != "pe"):
                    need[d] = True
        cnt = {e: 0 for e in self.ENG}
        opcnt = [0] * len(ops)
        dma_cnt = {}
        dma_after = [0] * len(ops)
        for i, o in enumerate(ops):
            if o["dma"] is not None:
                k = o["dma"]
                dma_cnt[k] = dma_cnt.get(k, 0) + 16
                dma_after[i] = dma_cnt[k]
            elif need[i]:
                cnt[o["eng"]] += 1
                opcnt[i] = cnt[o["eng"]]
        with contextlib.ExitStack() as st:
            sem = {e: st.enter_context(nc.semaphore("s_" + e)) for e in self.ENG}
            dsem = {k: st.enter_context(nc.semaphore("d_%d" % n)) for n, k in enumerate(dma_cnt)}
            waited = {e: {} for e in self.ENG}
            issued = {}
            nw = 0
            for i, o in enumerate(ops):
                e = o["eng"]
                E = engs[e]
                for d in sorted(o["deps"]):
                    od = ops[d]
                    if od["dma"] is not None:
                        key = ("d", od["dma"]); val = issued.get(od["dma"], 0); s = dsem[od["dma"]]
                    else:
                        if od["eng"] == e and o["dma"] is None and e == "pe":
                            continue
                        key = ("e", od["eng"]); val = opcnt[d]; s = sem[od["eng"]]
                    if waited[e].get(key, 0) >= val:
                        continue
                    waited[e][key] = val
                    E.wait_ge(s, val)
                    nw += 1
                ins = o["fn"](E)
                try:
                    o["iname"] = str(ins.ins.name)
                except Exception:
                    o["iname"] = None
                if o["dma"] is not None:
                    issued[o["dma"]] = dma_after[i]
                    ins.then_inc(dsem[o["dma"]], 16)
                elif need[i]:
                    ins.then_inc(sem[e], 1)
            for k, v in dma_cnt.items():
                nc.sync.wait_ge(dsem[k], v)
            self.stats = dict(n_ops=len(ops), n_waits=nw, cnt=cnt, ndsem=len(dma_cnt))


class Pool8:
    def __init__(self, n):
        self.free = list(range(n))

    def get(self):
        return self.free.pop(0)

    def put(self, i):
        self.free.append(i)


def build_program():
    nc = bass.Bass("TRN2", target_bir_lowering=False)
    st = contextlib.ExitStack()
    P = Prog(nc)

    def din(name, shape):
        return nc.dram_tensor(name, list(shape), F32, kind="ExternalInput").ap()

    def dout(name, shape):
        return nc.dram_tensor(name, list(shape), F32, kind="ExternalOutput").ap()

    x_p = din("x_p", [SEQ, D]); x_s = din("x_s", [128, D])
    p_p = din("p_p", [SEQ, 256]); p_s = din("p_s", [128, 256])
    st_conf = din("st_conf", [DEC_PER * 30, CW]); st_dn = din("st_dn", [DEC_PER * 3, 3 * DNW])
    st_S = din("st_S", [DEC_PER, 4, 128, 128]); st_ffn = din("st_ffn", [DEC_PER * 2, 2 * DFF])
    g_mix = din("g_mix", [8, 128]); w_in = din("w_in", [D, INC]); cw_w = din("cw_w", [CK * 4, 128])
    cw_b = din("cw_b", [4, 128]); ln_g = din("ln_g", [4, 128]); ln_b = din("ln_b", [4, 128])
    dn_w = din("dn_w", [DK * 12, 128]); a_log = din("a_log", [1, 4]); dt_b = din("dt_b", [1, 4])
    dn_g = din("dn_g", [1, 128]); w_out = din("w_out", [D, D]); g_ffn = din("g_ffn", [8, 128])
    w_up = din("w_up", [D, 2 * DFF]); f_w = din("f_w", [FK * 44, 128]); f_b = din("f_b", [44, 128])
    w_down = din("w_down", [DFF, D]); w_ple = din("w_ple", [256, D]); w_pg = din("w_pg", [D, D])
    g_fin = din("g_fin", [1, D])

    y_p = dout("y_p", [SEQ, D]); y_s = dout("y_s", [128, D])
    o_conf_p = dout("o_conf_p", [30, CW]); o_dn_p = dout("o_dn_p", [3, 3 * DNW])
    o_S_p = dout("o_S_p", [4, 128, 128]); o_ffn_p = dout("o_ffn_p", [2, 2 * DFF])
    o_conf_s = dout("o_conf_s", [DEC_PER, 30, CW]); o_dn_s = dout("o_dn_s", [DEC_PER, 3, 3 * DNW])
    o_S_s = dout("o_S_s", [DEC_PER, 4, 128, 128]); o_ffn_s = dout("o_ffn_s", [DEC_PER * 2, 2 * DFF])

    def sb(name, shape, dt=F32):
        return st.enter_context(nc.sbuf_tensor(name, list(shape), dt))

    pb = [st.enter_context(nc.psum_tensor("pb%d" % i, [128, 512], F32)) for i in range(8)]
    pbp = Pool8(8)

    def pk(i):
        return ("pb", i)

    def pbf(i):
        return pb[i][:].bitcast(BF16)

    def mm(out, lhsT, rhs, start, stop, reads, writes):
        P.op("pe", lambda e: e.matmul(out, lhsT=lhsT, rhs=rhs, start=start, stop=stop), reads=reads, writes=writes)

    def tr(out, in_, ident, reads, writes):
        P.op("pe", lambda e: e.transpose(out, in_, ident), reads=reads, writes=writes)

    def act(out, in_, func, reads, writes, bias=None, scale=None, accum=None):
        kw = {}
        if bias is not None:
            kw["bias"] = bias
        if scale is not None:
            kw["scale"] = scale
        if accum is not None:
            kw["accum_out"] = accum
        P.op("act", lambda e: e.activation(out=out, in_=in_, func=func, **kw), reads=reads, writes=writes)

    def tt(eng, out, in0, in1, op, reads, writes):
        P.op(eng, lambda e: e.tensor_tensor(out=out, in0=in0, in1=in1, op=op), reads=reads, writes=writes)

    def stt(eng, out, in0, scalar, in1, op0, op1, reads, writes):
        P.op(eng, lambda e: e.scalar_tensor_tensor(out=out, in0=in0, scalar=scalar, in1=in1, op0=op0, op1=op1),
             reads=reads, writes=writes)

    def ts(eng, out, in0, s1, s2, op0, op1, reads, writes):
        if op1 is None:
            P.op(eng, lambda e: e.tensor_scalar(out=out, in0=in0, scalar1=s1, scalar2=None, op0=op0), reads=reads, writes=writes)
        else:
            P.op(eng, lambda e: e.tensor_scalar(out=out, in0=in0, scalar1=s1, scalar2=s2, op0=op0, op1=op1),
                 reads=reads, writes=writes)

    def cp(eng, out, in_, reads, writes):
        if eng == "act":
            P.op("act", lambda e: e.activation(out=out, in_=in_, func=AF.Identity), reads=reads, writes=writes)
        else:
            P.op(eng, lambda e: e.tensor_copy(out=out, in_=in_), reads=reads, writes=writes)

    def pw(out, in_, n, reads, writes):
        P.op("pool", lambda e: e.tensor_tensor(out=out, in0=in_, in1=mhalf[:].to_broadcast([128, n]), op=ALU.pow),
             reads=list(reads) + ["mhalf"], writes=writes)

    def recip(out, in_, reads, writes):
        P.op("dve", lambda e: e.reciprocal(out=out, in_=in_), reads=reads, writes=writes)

    def dma(q, out, in_, reads, writes, key):
        P.op(q, lambda e: e.dma_start(out=out, in_=in_), reads=reads, writes=writes, dma=key)

    def memset(eng, ap, val, writes):
        P.op(eng, lambda e: e.memset(ap, val), writes=writes)

    def asel(out, in_, pattern, op, fill, base, cm, reads, writes):
        P.op("pool", lambda e: e.affine_select(out=out, in_=in_, pattern=pattern, compare_op=op, fill=fill, base=base,
                                               channel_multiplier=cm), reads=reads, writes=writes)

    evq = [0]

    def evac_eng():
        evq[0] += 1
        return "act" if evq[0] % 2 else "dve"

    ident_f = sb("ident_f", [128, 128]); ident_b = sb("ident_b", [128, 128], BF16)
    ones_f = sb("ones_f", [128, 128]); ones_b = sb("ones_b", [128, 128], BF16); odiv_f = sb("odiv_f", [128, 128])
    Ms = [sb("Ms_p", [128, 128]), sb("Ms_s", [128, 128])]
    Um = [sb("U_p", [128, 128]), sb("U_s", [128, 128])]
    NEGT = [sb("NEGT_p", [128, 128], BF16), sb("NEGT_s", [128, 128], BF16)]
    NEGS = [sb("NEGS_p", [128, 128], BF16), sb("NEGS_s", [128, 128], BF16)]
    Bsum = [ones_f, sb("Bsum_s", [128, 128])]
    Bsel = sb("Bsel", [128, 16])
    mhalf = sb("mhalf", [128, 1])
    memset("pool", ones_f[:], 1.0, ["ones_f"])
    memset("pool", mhalf[:], -0.5, ["mhalf"])
    memset("pool", odiv_f[:], 1.0 / CW, ["odiv_f"])
    cp("pool", ones_b[:], ones_f[:], ["ones_f"], ["ones_b"])
    asel(ident_f[:], ones_f[:], [[-1, 128]], ALU.is_equal, 0.0, 0, 1, ["ones_f"], ["ident_f"])
    cp("pool", ident_b[:], ident_f[:], ["ident_f"], ["ident_b"])
    asel(Ms[0][:], ones_f[:], [[-1, 128]], ALU.is_ge, 0.0, -1, 1, ["ones_f"], ["Ms0"])
    asel(Um[0][:], ones_f[:], [[1, 128]], ALU.is_ge, 0.0, 0, -1, ["ones_f"], ["U0"])
    v3 = lambda t: t[:].rearrange("p (a b) -> p a b", b=8)
    asel(v3(Bsum[1]), v3(ones_f), [[-8, 16], [0, 8]], ALU.is_ge, 0.0, 0, 1, ["ones_f"], ["Bsum1"])
    asel(v3(Bsum[1]), v3(Bsum[1]), [[8, 16], [0, 8]], ALU.is_ge, 0.0, 7, -1, ["Bsum1"], ["Bsum1"])
    asel(Bsel[:], ones_f[:, 0:16], [[-8, 16]], ALU.is_ge, 0.0, 0, 1, ["ones_f"], ["Bsel"])
    asel(Bsel[:], Bsel[:], [[8, 16]], ALU.is_ge, 0.0, 7, -1, ["Bsel"], ["Bsel"])
    tt("pool", Ms[1][:], Ms[0][:], Bsum[1][:], ALU.mult, ["Ms0", "Bsum1"], ["Ms1"])
    tt("pool", Um[1][:], Um[0][:], Bsum[1][:], ALU.mult, ["U0", "Bsum1"], ["U1"])
    for v in (0, 1):
        ts("pool", NEGS[v][:], Ms[v][:], -1.0, -NEG, ALU.add, ALU.mult, ["Ms%d" % v], ["NEGS%d" % v])
        ts("pool", NEGT[v][:], Um[v][:], -1.0, -NEG, ALU.add, ALU.mult, ["U%d" % v], ["NEGT%d" % v])

    colsT = sb("colsT", [128, 73])
    cwT = sb("cwT", [128, CK * 4])
    dnwT = sb("dnwT", [128, DK * 12])
    fwT = sb("fwT", [128, FK * 44])
    alog_bc = sb("alog_bc", [128, 4]); dtb_bc = sb("dtb_bc", [128, 4]); negA = sb("negA", [128, 4])
    gfin_bc = sb("gfin_bc", [128, D])
    stage = gfin_bc[:, 0:128]
    wtail = sb("wtail", [128, 8, 8], BF16)
    wple_sb = sb("wple_sb", [128, 2, D], BF16)

    lc_n = [0]

    def load_cols(srcs, dst, dcol):
        r = 0
        kq = lc_n[0]; lc_n[0] += 1
        stage = gfin_bc[:, kq * 128:(kq + 1) * 128]
        for (ap, n) in srcs:
            dma("sp", stage[r:r + n, :], ap, [], ["stage"], ("stage", kq))
            r += n
        b = pbp.get()
        tr(pb[b][:, 0:r], stage[0:r, :], ident_f[0:r, 0:r], ["stage", "ident_f"], [pk(b)])
        cp("dve", dst[:, dcol:dcol + r], pb[b][:, 0:r], [pk(b)], [dst.name])
        pbp.put(b)

    load_cols([(g_mix, 8), (g_ffn, 8), (cw_b, 4), (ln_g, 4), (ln_b, 4), (f_b, 44), (dn_g, 1)], colsT, 0)
    load_cols([(cw_w, 124)], cwT, 0)
    load_cols([(dn_w, 48)], dnwT, 0)
    load_cols([(f_w[0:128, :], 128)], fwT, 0)
    load_cols([(f_w[128:132, :], 4)], fwT, 128)
    dma("sp", alog_bc[:], a_log.to_broadcast([128, 4]), [], ["alog_bc"], "alog_bc")
    dma("sp", dtb_bc[:], dt_b.to_broadcast([128, 4]), [], ["dtb_bc"], "dtb_bc")
    dma("sp", gfin_bc[:], g_fin.to_broadcast([128, D]), [], ["gfin_bc", "stage"], "gfin_bc")
    dma("pool", wtail[:], w_in[:, 3072:3080].rearrange("(k p) c -> p k c", p=128), [], ["wtail"], "wtail")
    dma("pool", wple_sb[:], w_ple.rearrange("(k p) c -> p k c", p=128), [], ["wple_sb"], "wple_sb")
    act(negA[:], alog_bc[:], AF.Exp, ["alog_bc"], ["negA"])
    ts("dve", negA[:], negA[:], -1.0, None, ALU.mult, None, ["negA"], ["negA"])
    gmixT = colsT[:, 0:8]; gffnT = colsT[:, 8:16]
    cwbT = colsT[:, 16:20]; lngT = colsT[:, 20:24]; lnbT = colsT[:, 24:28]; fbT = colsT[:, 28:72]; dngT = colsT[:, 72:73]
    cw3 = cwT[:].rearrange("p (j k) -> p j k", k=4)
    dnw3 = dnwT[:].rearrange("p (j k) -> p j k", k=12)
    fw3 = fwT[:].rearrange("p (j k) -> p j k", k=44)

    xt = sb("xt", [128, 4, D])
    xrest = xt[:, 1:4, :].rearrange("p a b -> p (a b)")
    hT = sb("hT", [128, 8, 512], BF16)
    mixT = sb("mixT", [128, 8, 512], BF16)
    hbs = [sb("hb%d" % i, [128, D], BF16) for i in range(2)]
    hb = hbs[0]
    cbp = sb("cbp", [128, 4, 30 + 512], BF16)
    cbs = xrest[:, 1408:1408 + 1216].bitcast(BF16).rearrange("p (k s j) -> p k s j", k=4, s=16)
    uniB = sb("uniB", [128, 4 * 512])
    convf = uniB[:].rearrange("p (k n) -> p k n", n=512)
    sqt = [sb("sqt%d" % i, [128, 512]) for i in range(2)]
    junk = sqt[0][:].bitcast(BF16)
    gluf_p = sb("gluf_p", [128, 4, 32]); gluf_s = sb("gluf_s", [128, 4, 128])
    sig = [sb("sig%d" % i, [128, 512]) for i in range(2)]
    qbp = [sb("qbp%d" % i, [128, 3 + 512], BF16) for i in range(2)]
    qbs = [sb("qbs%d" % i, [128, 16, 11], BF16) for i in range(2)]
    qhist = sb("qhist", [128, 12, 3], BF16)
    qhist_s = sb("qhist_s", [128, 12, 16, 3], BF16)
    qkvf_p = sb("qkvf_p", [128, 12, 4]); qkvf_s = sb("qkvf_s", [128, 12, 16, 3])
    qkvT = sb("qkvT", [128, 12, 512], BF16)
    szT = sb("szT", [128, 4, 512], BF16)
    ba = sb("ba", [128, 4, 8])
    uniC = sb("uniC", [128, 2 * 4 * 514], BF16)
    ubp = [uniC[:, i * 2056:(i + 1) * 2056].rearrange("p (q n) -> p q n", n=514) for i in range(2)]
    ubs = [sb("ubs%d" % i, [128, 4, 16, 10], BF16) for i in range(2)]
    uhist = sb("uhist", [128, 44, 2], BF16)
    uhist_s = sb("uhist_s", [128, 44, 16, 2], BF16)
    upf_p = sb("upf_p", [128, 44, 4]); upf_s = xrest[:, 0:1408].rearrange("p (c n) -> p c n", n=32)
    uniA = sb("uniA", [128, 22 * 512], BF16)
    actT = uniA[:].rearrange("p (k n) -> p k n", n=512)
    NW = 3
    wr = [sb("wr%d" % i, [128, 8, 512], BF16) for i in range(NW)]
    wrp = Pool8(NW)
    dgc = [sb("dgc%d" % i, [128, 8, 128], BF16) for i in range(2)]
    dgq = [sb("dgq%d" % i, [128, 4, 128], BF16) for i in range(2)]
    dgf = [sb("dgf%d" % i, [128, 4, 3, 128], BF16) for i in range(2)]
    small = sb("small", [128, 64])
    ptile4 = sb("ptile4", [128, 4, 256], BF16); pT = sb("pT", [128, 2, 512], BF16)
    tmqk = sb("tmqk", [128, 8, 128], BF16); tmv = sb("tmv", [128, 4, 128], BF16)
    gU = sb("gU", [128, 4, 128]); ETm = sb("ETm", [128, 4, 128]); Em = sb("Em", [128, 4, 128])
    A_bf = sb("A_bf", [128, 4, 128], BF16); qkT = sb("qkT", [128, 4, 128], BF16)
    Pm = [sb("Pm%d" % i, [128, 4, 128], BF16) for i in range(2)]
    PTm = [sb("PTm%d" % i, [128, 4, 128], BF16) for i in range(2)]
    Rm = [sb("Rm%d" % i, [128, 4, 128], BF16) for i in range(2)]
    PT3 = sb("PT3", [128, 4, 128], BF16)
    vb = sb("vb", [128, 4, 128], BF16); kbe = sb("kbe", [128, 4, 128], BF16); kdec = sb("kdec", [128, 4, 128], BF16)
    qdec = sb("qdec", [128, 4, 128], BF16); qdT = sb("qdT", [128, 4, 128], BF16); wT = sb("wT", [128, 4, 128], BF16)
    u_sb = sb("u_sb", [128, 4, 128]); vnew = sb("vnew", [128, 4, 128], BF16); o_sb = sb("o_sb", [128, 4, 128])
    on_bf = sb("on_bf", [128, 4, 128], BF16)
    S_f = sb("S_f", [128, 4, 128]); S_bf = sb("S_bf", [128, 4, 128], BF16)
    coef = sb("coef", [128, 40])
    cf = sb("cf", [128, 9, 16])
    S0f = uniA[:, 0:4096].bitcast(F32).rearrange("p (s v) -> p s v", v=128)
    Sout = uniA[:, 4096:8192].bitcast(F32).rearrange("p (s v) -> p s v", v=128)
    S0b = uniA[:, 8192:10240].rearrange("p (s v) -> p s v", v=128)
    kdx = uniA[:, 10240:11264].rearrange("p (s v) -> p s v", v=128)
    wTx = uniB[:, 0:1088].bitcast(BF16)
    qdTx = uniC[:, 0:2176]
    _upf_flat = upf_p[:].rearrange("p c n -> p (c n)")
    GB = _upf_flat[:, 0:64].rearrange("p (s h) -> p s h", h=4); egts = _upf_flat[:, 64:128].rearrange("p (s h) -> p s h", h=4)
    sttile = sig[0]
    orow = sig[1]

    memset("dve", small[:], 0.0, ["small", "stsrc"] + [("small", t) for t in range(4)] + [("smallf", t) for t in range(4)])
    memset("dve", S_f[:], 0.0, ["S_f"])
    memset("dve", S_bf[:], 0.0, ["S_bf"])
    memset("dve", cbp[:, :, 0:30], 0.0, ["cbp"])
    memset("dve", qhist[:], 0.0, ["qhist"])
    memset("dve", uhist[:], 0.0, ["uhist"])

    def wspecs():
        for blk in range(5):
            for half in range(2):
                yield [(w_in[:, half * 256:(half + 1) * 256], 8, 0, 256),
                       (w_in[:, 512 + half * 256:512 + (half + 1) * 256], 8, 256, 256)]
            for wg in range(4):
                yield [(w_in[:, 1024 + wg * 512:1024 + (wg + 1) * 512], 8, 0, 512)]
            for half in range(2):
                yield [(w_out[:, half * 512:(half + 1) * 512], 8, 0, 512)]
            for i in range(11):
                yield [(w_up[:, i * 256:(i + 1) * 256], 8, 0, 256), (w_up[:, DFF + i * 256:DFF + (i + 1) * 256], 8, 256, 256)]
            for half in range(2):
                for (k0, nk) in ((0, 8), (8, 8), (16, 6)):
                    yield [(w_down[k0 * 128:(k0 + nk) * 128, half * 512:(half + 1) * 512], nk, 0, 512)]
            for half in range(2):
                yield [(w_pg[:, half * 512:(half + 1) * 512], 8, 0, 512)]

    wit = wspecs()
    wq = []

    def wfill():
        while len(wq) + len(wr) - len(wrp.free) - len(wq) < NW and wrp.free:
            parts = next(wit, None)
            if parts is None:
                return
            s = wrp.get()
            for (ap, kc, off, ncols) in parts:
                if len(ap.shape) == 3:
                    dma("pool", wr[s][:, 0:kc, :].rearrange("p k (g c) -> p k g c", g=2), ap.rearrange("(k p) g c -> p k g c", p=128),
                        [], [("wr", s)], ("wr", s))
                else:
                    dma("pool", wr[s][:, 0:kc, off:off + ncols], ap.rearrange("(k p) c -> p k c", p=128), [], [("wr", s)], ("wr", s))
            wq.append(s)

    def wload(parts=None):
        if not wq:
            wfill()
        s = wq.pop(0)
        wfill()
        return s

    def norm_T(t, gT, dst, key_dst, do_norm=True):
        norm_front(t, do_norm)
        norm_back(t, gT, dst, key_dst)

    def norm_front(t, do_norm=True):
        xk = ("xt", t)
        hb = hbs[t % 2]; hk = ("hb", t % 2)
        sm = small[:, t * 5:t * 5 + 5]; smk = ("small", t)
        if do_norm:
            act(junk[:], xt[:, t, :], AF.Square, [xk], [("sqt", 0), smk], accum=sm[:, 0:1])
            ts("dve", sm[:, 2:3], sm[:, 0:1], 1.0 / D, EPS, ALU.mult, ALU.add, [smk], [smk])
            pw(sm[:, 4:5], sm[:, 2:3], 1, [smk], [smk])
            ts("dve", hb[:], xt[:, t, :], sm[:, 4:5], None, ALU.mult, None, [xk, smk], [hk])
        else:
            cp("dve", hb[:], xt[:, t, :], [xk], [hk])

    def norm_back(t, gT, dst, key_dst):
        hb = hbs[t % 2]; hk = ("hb", t % 2)
        b = pbp.get()
        for kc in range(8):
            tr(pbf(b)[:, kc * 128:(kc + 1) * 128], hb[:, kc * 128:(kc + 1) * 128], ident_b[:], [hk, "ident_b"], [pk(b)])
        src = pbf(b).rearrange("p (k c) -> p k c", c=128)
        if gT is not None:
            tt("dve", dst[:, :, t * 128:(t + 1) * 128], src, gT.unsqueeze(2).to_broadcast([128, 8, 128]), ALU.mult,
               [pk(b), "colsT"], [key_dst, ("hTj", t)])
        else:
            cp("act", dst[:, :, t * 128:(t + 1) * 128], src, [pk(b)], [key_dst, ("hTj", t)])
        pbp.put(b)

    pre_a = [False]

    def _hg_dma(kind, ci):
        stg = sqt[1]
        if kind == 0:
            dma("sp", stg[0:48, :], st_dn[:, ci * 512:(ci + 1) * 512], [], [("sqt", 1)], "hstage")
        else:
            dma("sp", stg[0:32, :], st_ffn[:, ci * 512:(ci + 1) * 512], [], [("sqt", 1)], "hstage")

    def _hg_tr(kind, ci):
        stg = sqt[1]
        for k in range(4):
            b = pbp.get()
            if kind == 0:
                tr(pb[b][:, 0:48], stg[0:48, k * 128:(k + 1) * 128], ident_f[0:48, 0:48], [("sqt", 1), "ident_f"], [pk(b)])
                cp("dve", qhist_s[:, ci * 4 + k, :, :], pb[b][:, 0:48].rearrange("p (s j) -> p s j", j=3), [pk(b)], ["qhist_s"])
            else:
                tr(pb[b][:, 0:32], stg[0:32, k * 128:(k + 1) * 128], ident_f[0:32, 0:32], [("sqt", 1), "ident_f"], [pk(b)])
                cp("dve", uhist_s[:, ci * 4 + k, :, :], pb[b][:, 0:32].rearrange("p (s j) -> p s j", j=2), [pk(b)], ["uhist_s"])
            pbp.put(b)

    hgroups = [(0, c3) for c3 in range(3)] + [(1, c11) for c11 in range(11)]
    hpend = []

    def hist_step():
        if hpend:
            _hg_tr(*hpend.pop(0))
        if hgroups:
            g = hgroups.pop(0)
            _hg_dma(*g)
            hpend.append(g)

    def do_block(blk):
        samp = blk == 4
        NT = 128 if samp else 512
        ntile = NT // 128
        v = 1 if samp else 0
        nseq, L = (16, 8) if samp else (1, 512)
        xsrc = x_s if samp else x_p[blk * 512:(blk + 1) * 512, :]
        psrc = p_s if samp else p_p[blk * 512:(blk + 1) * 512, :]
        last_p = blk == 3

        if not pre_a[0]:
            for t in range(ntile):
                dma("sp", xt[:, t, :], xsrc[t * 128:(t + 1) * 128, :], [], [("xt", t)], ("xt", t))
            for t in range(ntile):
                norm_T(t, gmixT, hT, "hT")
        pre_a[0] = False
        for t in range(ntile):
            dma("pool", ptile4[:, t, :], psrc[t * 128:(t + 1) * 128, :], [], [("ptile", t)], ("ptile", t))

        if samp:
            for r in range(4):
                dma("sp", sttile[0:120, :], st_conf[r * 120:(r + 1) * 120, :], [], [("sig", 0)], "sttile")
                for k in range(4):
                    b = pbp.get()
                    tr(pb[b][:, 0:120], sttile[0:120, k * 128:(k + 1) * 128], ident_f[0:120, 0:120], [("sig", 0), "ident_f"], [pk(b)])
                    cp("dve", cbs[:, k, r * 4:(r + 1) * 4, 0:30], pb[b][:, 0:120].rearrange("p (s j) -> p s j", j=30), [pk(b)], ["cbs"])
                    pbp.put(b)

        def conv_view(buf3, j):
            if samp:
                return buf3[:, :, j:j + L]
            return buf3[:, j:j + L]

        def outv(ap2):
            if samp:
                return ap2.rearrange("p (s l) -> p s l", l=L)
            return ap2

        cb_k = "cbs" if samp else "cbp"
        for half in range(2):
            s = wload([(w_in[:, half * 256:(half + 1) * 256], 8, 0, 256),
                       (w_in[:, 512 + half * 256:512 + (half + 1) * 256], 8, 256, 256)])
            for a in range(2):
                k = half * 2 + a
                bg = pbp.get(); bv = pbp.get()
                for kc in range(8):
                    mm(pb[bg][:, 0:NT], wr[s][:, kc, 256 + a * 128:256 + (a + 1) * 128], hT[:, kc, 0:NT], kc == 0, kc == 7,
                       [("wr", s), "hT"], [pk(bg)])
                for kc in range(8):
                    mm(pb[bv][:, 0:NT], wr[s][:, kc, a * 128:(a + 1) * 128], hT[:, kc, 0:NT], kc == 0, kc == 7,
                       [("wr", s), "hT"], [pk(bv)])
                sg = sig[k % 2]; sgk = ("sig", k % 2)
                act(sg[:, 0:NT], pb[bg][:, 0:NT], AF.Sigmoid, [pk(bg)], [sgk])
                pbp.put(bg)
                if samp:
                    tt("dve", cbs[:, k, :, 30:38], outv(pb[bv][:, 0:NT]), outv(sg[:, 0:NT]), ALU.mult, [pk(bv), sgk], [cb_k])
                    tt("dve", gluf_s[:, k, :], pb[bv][:, 0:NT], sg[:, 0:NT], ALU.mult, [pk(bv), sgk], ["gluf_s"])
                else:
                    tt("dve", cbp[:, k, 30:30 + L], pb[bv][:, 0:NT], sg[:, 0:NT], ALU.mult, [pk(bv), sgk], [cb_k])
                    if last_p:
                        tt("dve", gluf_p[:, k, :], pb[bv][:, NT - 32:NT], sg[:, NT - 32:NT], ALU.mult, [pk(bv), sgk], ["gluf_p"])
                pbp.put(bv)
            wrp.put(s)

        pconv = []

        def qconv(c, qi, d):
            qk_ = ("qb", qi)
            b2 = pbp.get()
            for j in range(DK):
                src = conv_view(qbs[qi][:] if samp else qbp[qi][:], j)
                mm(outv(pb[b2][:, 0:NT]), dgq[d][:, j, :], src, j == 0, j == DK - 1, [("dgq", d), (qk_, "h"), (qk_, "d")], [pk(b2)])
            act(qkvT[:, c, 0:NT], pb[b2][:, 0:NT], AF.Silu, [pk(b2)], [("qkvT", c)])
            pbp.put(b2)
            if not samp:
                cp("pool", qhist[:, c, :], qbp[qi][:, L:L + 3], [(qk_, "d")], ["qhist"])

        for wg in range(4):
            s = wload()
            for cc in range(4):
                c = wg * 4 + cc
                qi = c % 2
                qk_ = ("qb", qi)
                d = c % 2
                if wg < 3:
                    if samp:
                        cp("pool", qbs[qi][:, :, 0:3], qhist_s[:, c, :, :], ["qhist_s"], [(qk_, "h")])
                    else:
                        cp("pool", qbp[qi][:, 0:3], qhist[:, c, :], ["qhist"], [(qk_, "h")])
                    tt("pool", dgq[d][:], ident_b[:].unsqueeze(1).to_broadcast([128, 4, 128]),
                       dnw3[:, :, c].unsqueeze(2).to_broadcast([128, 4, 128]), ALU.mult, ["ident_b", "dnwT"], [("dgq", d)])
                b = pbp.get()
                for kc in range(8):
                    mm(pb[b][:, 0:NT], wr[s][:, kc, cc * 128:(cc + 1) * 128], hT[:, kc, 0:NT], kc == 0, kc == 7,
                       [("wr", s), "hT"], [pk(b)])
                if pconv:
                    pconv.pop(0)()
                if wg == 3:
                    act(szT[:, cc, 0:NT], pb[b][:, 0:NT], AF.Silu, [pk(b)], ["szT"])
                    pbp.put(b)
                    continue
                if samp:
                    cp(evac_eng(), qbs[qi][:, :, 3:11], outv(pb[b][:, 0:NT]), [pk(b)], [(qk_, "d")])
                    cp("dve", qkvf_s[:, c, :, :], outv(pb[b][:, 0:NT])[:, :, 5:8], [pk(b)], ["qkvf_s"])
                else:
                    cp(evac_eng(), qbp[qi][:, 3:3 + L], pb[b][:, 0:NT], [pk(b)], [(qk_, "d")])
                    if last_p:
                        cp("dve", qkvf_p[:, c, :], pb[b][:, NT - 4:NT], [pk(b)], ["qkvf_p"])
                pbp.put(b)
                pconv.append(lambda c=c, qi=qi, d=d: qconv(c, qi, d))
            wrp.put(s)
        while pconv:
            pconv.pop(0)()
        for t in range(ntile):
            b = pbp.get()
            for kc in range(8):
                mm(pb[b][:, 0:8], hT[:, kc, t * 128:(t + 1) * 128], wtail[:, kc, :], kc == 0, kc == 7, ["hT", "wtail"], [pk(b)])
            cp("dve", ba[:, t, :], pb[b][:, 0:8], [pk(b)], [("ba", t)])
            pbp.put(b)

        n4 = ntile * 4
        bav = ba[:, 0:ntile, :]
        baks = [("ba", t) for t in range(ntile)]
        cfv = lambda kind: cf[:, kind, 0:n4]
        cf3 = lambda kind: cf[:, kind, 0:n4].rearrange("p (t h) -> p t h", h=4)
        act(cf3(0), bav[:, :, 0:4], AF.Sigmoid, baks, ["cf"])
        tt("dve", cf3(7), bav[:, :, 4:8], dtb_bc[:].unsqueeze(1).to_broadcast([128, ntile, 4]), ALU.add, baks + ["dtb_bc"], ["cf"])
        act(cfv(7), cfv(7), AF.Exp, ["cf"], ["cf"])
        ts("dve", cfv(7), cfv(7), 1.0, None, ALU.add, None, ["cf"], ["cf"])
        act(cfv(7), cfv(7), AF.Ln, ["cf"], ["cf"])
        tt("dve", cf3(1), cf3(7), negA[:].unsqueeze(1).to_broadcast([128, ntile, 4]), ALU.mult, ["cf", "negA"], ["cf"])
        b = pbp.get()
        for t in range(ntile):
            mm(pb[b][:, t * 4:(t + 1) * 4], Um[v][:], cf[:, 1, t * 4:(t + 1) * 4], True, True, ["U%d" % v, "cf"], [pk(b)])
        mm(pb[b][:, 32:32 + n4], Bsum[v][:], cfv(1), True, True, ["Bsum1", "ones_f", "cf"], [pk(b)])
        act(cfv(2), pb[b][:, 0:n4], AF.Exp, [pk(b)], ["cf"])
        cp("dve", cfv(8), pb[b][:, 32:32 + n4], [pk(b)], ["cf"])
        tt("dve", cfv(7), cfv(8), pb[b][:, 0:n4], ALU.subtract, ["cf", pk(b)], ["cf"])
        pbp.put(b)
        act(cfv(3), cfv(7), AF.Exp, ["cf"], ["cf"])
        act(cfv(6), cfv(8), AF.Exp, ["cf"], ["cf"])
        tt("dve", cfv(4), cfv(0), cfv(2), ALU.mult, ["cf"], ["cf"])
        ts("dve", cfv(5), cfv(2), 128.0 ** -0.5, None, ALU.mult, None, ["cf"], ["cf"])

        def dg_build(k, g):
            j0 = g * 8
            nj = 8 if g < 3 else 7
            d = g % 2
            tt("pool", dgc[d][:, 0:nj, :], ident_b[:].unsqueeze(1).to_broadcast([128, nj, 128]),
               cw3[:, j0:j0 + nj, k].unsqueeze(2).to_broadcast([128, nj, 128]), ALU.mult, ["ident_b", "cwT"], [("dgc", d)])

        def dg_mm(k, g, b):
            j0 = g * 8
            nj = 8 if g < 3 else 7
            d = g % 2
            for jj in range(nj):
                j = j0 + jj
                src = conv_view(cbs[:, k] if samp else cbp[:, k], j)
                mm(outv(pb[b][:, 0:NT]), dgc[d][:, jj, :], src, j == 0, j == CK - 1, [("dgc", d), cb_k], [pk(b)])

        def l2_a(t, bA):
            tc_ = slice(t * 128, (t + 1) * 128)
            sqb = hbs[0]
            for c in range(8):
                tr(pbf(bA)[:, c * 128:(c + 1) * 128], qkvT[:, c, tc_], ident_b[:], [("qkvT", cx) for cx in range(8)] + ["ident_b"], [pk(bA)])
            act(sqb[:], pbf(bA), AF.Square, [pk(bA)], [("hb", 0)])
            P.op("dve", lambda e, sqb=sqb: e.tensor_reduce(out=coef[:, 0:8], in_=sqb[:].rearrange("p (c d) -> p c d", d=128), axis=AX.X, op=ALU.add),
                 reads=[("hb", 0)], writes=["coef"])
            ts("dve", coef[:, 0:8], coef[:, 0:8], EPS, None, ALU.add, None, ["coef"], ["coef"])

        def l2_b(t, bA):
            tmn = hbs[1]
            pw(coef[:, 8:16], coef[:, 0:8], 8, ["coef"], ["coef"])
            tt("dve", tmn[:].rearrange("p (c d) -> p c d", d=128), pbf(bA).rearrange("p (c d) -> p c d", d=128),
               coef[:, 8:16].unsqueeze(2).to_broadcast([128, 8, 128]), ALU.mult, [pk(bA), "coef"], [("hb", 1)])

        def l2_c(t):
            tc_ = slice(t * 128, (t + 1) * 128)
            tmn = hbs[1]
            bB = pbp.get()
            for c in range(8):
                tr(pbf(bB)[:, c * 128:(c + 1) * 128], tmn[:, c * 128:(c + 1) * 128], ident_b[:], [("hb", 1), "ident_b"], [pk(bB)])
            cp("act", qkvT[:, 0:8, tc_], pbf(bB).rearrange("p (c d) -> p c d", d=128), [pk(bB)], [("qkvT", cx) for cx in range(8)])
            pbp.put(bB)

        for k in range(4):
            do_l2 = k < ntile
            if do_l2:
                bA = pbp.get()
                l2_a(k, bA)
            b = pbp.get()
            dg_build(k, 0)
            dg_build(k, 1)
            dg_mm(k, 0, b)
            dg_build(k, 2)
            dg_mm(k, 1, b)
            dg_build(k, 3)
            if do_l2:
                l2_b(k, bA)
                pbp.put(bA)
            dg_mm(k, 2, b)
            dg_mm(k, 3, b)
            act(convf[:, k, 0:NT], pb[b][:, 0:NT], AF.Identity, [pk(b), "colsT"], ["convf"], bias=cwbT[:, k:k + 1])
            pbp.put(b)
            if do_l2:
                l2_c(k)
        if not samp:
            cp("pool", cbp[:, :, 0:30], cbp[:, :, L:L + 30], ["cbp"], ["cbp"])

        bm = pbp.get(); bq = pbp.get()
        for k in range(4):
            mm(pb[bm][:, 0:NT], odiv_f[:], convf[:, k, 0:NT], k == 0, k == 3, ["odiv_f", "convf"], [pk(bm)])
        for k in range(4):
            sq = sqt[k % 2]; sqk = ("sqt", k % 2)
            act(sq[:, 0:NT], convf[:, k, 0:NT], AF.Square, ["convf"], [sqk])
            mm(pb[bq][:, 0:NT], odiv_f[:], sq[:, 0:NT], k == 0, k == 3, ["odiv_f", sqk], [pk(bq)])
        mean = sqt[0]; rst = sqt[1]
        cp("act", mean[:, 0:NT], pb[bm][:, 0:NT], [pk(bm)], [("sqt", 0)])
        act(sig[0][:, 0:NT], pb[bm][:, 0:NT], AF.Square, [pk(bm)], [("sig", 0)])
        tt("dve", sig[1][:, 0:NT], pb[bq][:, 0:NT], sig[0][:, 0:NT], ALU.subtract, [pk(bq), ("sig", 0)], [("sig", 1)])
        pbp.put(bm); pbp.put(bq)
        ts("dve", sig[1][:, 0:NT], sig[1][:, 0:NT], EPS, None, ALU.add, None, [("sig", 1)], [("sig", 1)])
        act(sig[0][:, 0:NT], sig[1][:, 0:NT], AF.Ln, [("sig", 1)], [("sig", 0)])
        act(rst[:, 0:NT], sig[0][:, 0:NT], AF.Exp, [("sig", 0)], [("sqt", 1)], scale=-0.5)
        tt("dve", convf[:, :, 0:NT], convf[:, :, 0:NT], mean[:, 0:NT].unsqueeze(1).to_broadcast([128, 4, NT]), ALU.subtract,
           ["convf", ("sqt", 0)], ["convf"])
        tt("dve", convf[:, :, 0:NT], convf[:, :, 0:NT], rst[:, 0:NT].unsqueeze(1).to_broadcast([128, 4, NT]), ALU.mult,
           ["convf", ("sqt", 1)], ["convf"])
        for k in range(4):
            act(mixT[:, k, 0:NT], convf[:, k, 0:NT], AF.Silu, ["convf", "colsT"], [("mixT", "c")], bias=lnbT[:, k:k + 1], scale=lngT[:, k:k + 1])

        otail = []
        sF = []
        fdone = [0]

        def f_half(t, half):
            if not sF:
                sF.append(wload()); sF.append(wload())
            b = pbp.get()
            for kc in range(8):
                mm(pb[b][:], mixT[:, kc, t * 128:(t + 1) * 128], wr[sF[half]][:, kc, :], kc == 0, kc == 7,
                   [("mixT", "c"), ("mixT", t), ("wr", sF[half])], [pk(b)])
            tt("dve", xt[:, t, half * 512:(half + 1) * 512], xt[:, t, half * 512:(half + 1) * 512], pb[b][:], ALU.add,
               [("xt", t), pk(b)], [("xt", t)])
            pbp.put(b)

        def f_tile(t):
            if not sF:
                sF.append(wload()); sF.append(wload())
            for half in range(2):
                b = pbp.get()
                for kc in range(8):
                    mm(pb[b][:], mixT[:, kc, t * 128:(t + 1) * 128], wr[sF[half]][:, kc, :], kc == 0, kc == 7,
                       [("mixT", "c"), ("mixT", t), ("wr", sF[half])], [pk(b)])
                tt("dve", xt[:, t, half * 512:(half + 1) * 512], xt[:, t, half * 512:(half + 1) * 512], pb[b][:], ALU.add,
                   [("xt", t), pk(b)], [("xt", t)])
                pbp.put(b)
            norm_T(t, gffnT, hT, "hT")
            fdone[0] = t + 1

        for t in range(ntile):
            tc_ = slice(t * 128, (t + 1) * 128)
            t4 = slice(t * 4, (t + 1) * 4)
            beta = cf[:, 0, t4]; G = cf[:, 1, t4]
            bc4 = lambda ap: ap.unsqueeze(2).to_broadcast([128, 4, 128])

            def build_gU(tt_):
                Gx = cf[:, 1, tt_ * 4:(tt_ + 1) * 4]
                tt("dve", gU[:], Um[v][:].unsqueeze(1).to_broadcast([128, 4, 128]), Gx.unsqueeze(2).to_broadcast([128, 4, 128]), ALU.mult,
                   ["U%d" % v, "cf"], ["gU"])
            if t == 0:
                build_gU(0)

            bT = pbp.get()
            for h in range(4):
                mm(pb[bT][:, h * 128:(h + 1) * 128], Ms[v][:], gU[:, h, :], True, False, ["Ms%d" % v, "gU"], [pk(bT)])
                mm(pb[bT][:, h * 128:(h + 1) * 128], ident_b[:], NEGT[v][:], False, True, ["ident_b", "NEGT%d" % v], [pk(bT)])
            act(ETm[:].rearrange("p h i -> p (h i)"), pb[bT][:], AF.Exp, [pk(bT)], ["ETm"])
            pbp.put(bT)
            bE = pbp.get()
            for h in range(4):
                mm(pb[bE][:, h * 128:(h + 1) * 128], gU[:, h, :], Ms[v][:], True, False, ["gU", "Ms%d" % v], [pk(bE)])
                mm(pb[bE][:, h * 128:(h + 1) * 128], ident_b[:], NEGS[v][:], False, True, ["ident_b", "NEGS%d" % v], [pk(bE)])
            act(Em[:].rearrange("p h i -> p (h i)"), pb[bE][:], AF.Exp, [pk(bE)], ["Em"])
            pbp.put(bE)
            tt("dve", Em[:], Em[:], bc4(cf[:, 0, t4]), ALU.mult, ["Em", "cf"], ["Em"])
            if t + 1 < ntile:
                build_gU(t + 1)
            bK = pbp.get(); bQ = pbp.get()
            for h in range(4):
                mm(pb[bK][:, h * 128:(h + 1) * 128], qkvT[:, 4 + h, tc_], qkvT[:, 4 + h, tc_], True, True, [("qkvT", cx) for cx in range(12)], [pk(bK)])
            for h in range(4):
                mm(pb[bQ][:, h * 128:(h + 1) * 128], qkvT[:, 4 + h, tc_], qkvT[:, h, tc_], True, True, [("qkvT", cx) for cx in range(12)], [pk(bQ)])
            b = pbp.get(); b2 = pbp.get()
            for c in range(8):
                tr(pbf(b)[:, c * 128:(c + 1) * 128], qkvT[:, c, tc_], ident_b[:], [("qkvT", cx) for cx in range(12)] + ["ident_b"], [pk(b)])
            for c in range(4):
                tr(pbf(b2)[:, c * 128:(c + 1) * 128], qkvT[:, 8 + c, tc_], ident_b[:], [("qkvT", cx) for cx in range(12)] + ["ident_b"], [pk(b2)])
            cp("act", tmqk[:], pbf(b).rearrange("p (c d) -> p c d", d=128), [pk(b)], ["tmqk"])
            cp("dve", tmv[:], pbf(b2)[:, 0:512].rearrange("p (c d) -> p c d", d=128), [pk(b2)], ["tmv"])
            pbp.put(b); pbp.put(b2)
            tt("pool", vb[:], tmv[:], bc4(cf[:, 0, t4]), ALU.mult, ["tmv", "cf"], ["vb"])
            tt("pool", kbe[:], tmqk[:, 4:8, :], bc4(cf[:, 4, t4]), ALU.mult, ["tmqk", "cf"], ["kbe"])
            tt("pool", kdec[:], tmqk[:, 4:8, :], bc4(cf[:, 3, t4]), ALU.mult, ["tmqk", "cf"], ["kdec"])
            tt("pool", qdec[:], tmqk[:, 0:4, :], bc4(cf[:, 5, t4]), ALU.mult, ["tmqk", "cf"], ["qdec"])

            tt("dve", A_bf[:].rearrange("p h i -> p (h i)"), pb[bK][:], Em[:].rearrange("p h i -> p (h i)"), ALU.mult, [pk(bK), "Em"], ["A_bf"])
            stt("dve", qkT[:].rearrange("p h i -> p (h i)"), pb[bQ][:], 128.0 ** -0.5, ETm[:].rearrange("p h i -> p (h i)"), ALU.mult, ALU.mult,
                [pk(bQ), "ETm"], ["qkT"])
            pbp.put(bK); pbp.put(bQ)
            b = pbp.get()
            for h in range(4):
                tr(pbf(b)[:, h * 128:(h + 1) * 128], A_bf[:, h, :], ident_b[:], ["A_bf", "ident_b"], [pk(b)])
            pv = pbf(b)[:, 0:512].rearrange("p (h i) -> p h i", i=128)
            cp("act", Pm[0][:], pv, [pk(b)], [("Pm", 0)])
            act(Rm[0][:], pv, AF.Identity, [pk(b)], [("Rm", 0)], scale=-1.0)
            tt("pool", Rm[0][:], Rm[0][:], ident_b[:].unsqueeze(1).to_broadcast([128, 4, 128]), ALU.add, ["ident_b", ("Rm", 0)], [("Rm", 0)])
            pbp.put(b)
            cur = 0
            rc = 0
            PTcur = A_bf
            PTkey = "A_bf"

            def r_update(PTl, PTlk, rc):
                bR = pbp.get()
                for h in range(4):
                    mm(pb[bR][:, h * 128:(h + 1) * 128], ident_b[:], Rm[rc][:, h, :], True, False, ["ident_b", ("Rm", rc)], [pk(bR)])
                    mm(pb[bR][:, h * 128:(h + 1) * 128], PTl[:, h, :], Rm[rc][:, h, :], False, True, [PTlk, ("Rm", rc)], [pk(bR)])
                cp(evac_eng(), Rm[1 - rc][:].rearrange("p h i -> p (h i)"), pb[bR][:], [pk(bR)], [("Rm", 1 - rc)])
                pbp.put(bR)

            PTbuf = [sttl for sttl in PTm] + [PT3]
            for lvl in range(1, 7):
                nxt = 1 - cur
                pti = lvl % 3
                bPT = pbp.get()
                for h in range(4):
                    mm(pb[bPT][:, h * 128:(h + 1) * 128], Pm[cur][:, h, :], PTcur[:, h, :], True, True, [("Pm", cur), PTkey], [pk(bPT)])
                if lvl < 6:
                    bP = pbp.get()
                    for h in range(4):
                        mm(pb[bP][:, h * 128:(h + 1) * 128], PTcur[:, h, :], Pm[cur][:, h, :], True, True, [("Pm", cur), PTkey], [pk(bP)])
                cp("act", PTbuf[pti][:].rearrange("p h i -> p (h i)"), pb[bPT][:], [pk(bPT)], [("PTm", pti)])
                pbp.put(bPT)
                if lvl < 6:
                    cp("dve", Pm[nxt][:].rearrange("p h i -> p (h i)"), pb[bP][:], [pk(bP)], [("Pm", nxt)])
                    pbp.put(bP)
                if lvl > 1:
                    r_update(PTcur, PTkey, rc)
                    rc = 1 - rc
                if t >= 1 and not samp:
                    if lvl == 1 and otail:
                        otail.pop(0)()
                    if lvl == 3:
                        f_half(t - 1, 0)
                    if lvl == 4:
                        f_half(t - 1, 1)
                        norm_front(t - 1)
                    if lvl == 6:
                        norm_back(t - 1, gffnT, hT, "hT")
                        fdone[0] = t
                PTcur = PTbuf[pti]; PTkey = ("PTm", pti)
                cur = nxt
            r_update(PTcur, PTkey, rc)
            rc = 1 - rc
            cur = rc
            if otail:
                otail.pop(0)()
            TT = Rm[cur]; TTk = ("Rm", cur)
            bu = pbp.get(); bw = pbp.get(); bq = pbp.get()
            for h in range(4):
                mm(pb[bu][:, h * 128:(h + 1) * 128], TT[:, h, :], vb[:, h, :], True, True, [TTk, "vb"], [pk(bu)])
            for h in range(4):
                mm(pb[bw][:, h * 128:(h + 1) * 128], kbe[:, h, :], TT[:, h, :], True, True, [TTk, "kbe"], [pk(bw)])
            for h in range(4):
                tr(pbf(bq)[:, h * 128:(h + 1) * 128], qdec[:, h, :], ident_b[:], ["qdec", "ident_b"], [pk(bq)])
            cp("act", u_sb[:].rearrange("p h i -> p (h i)"), pb[bu][:], [pk(bu)], ["u_sb"])
            cp("dve", wT[:].rearrange("p h i -> p (h i)"), pb[bw][:], [pk(bw)], ["wT"])
            cp("act", qdT[:].rearrange("p h i -> p (h i)"), pbf(bq)[:, 0:512], [pk(bq)], ["qdT"])
            pbp.put(bu); pbp.put(bw); pbp.put(bq)

            bo = pbp.get()
            if not samp:
                bv_ = pbp.get()
                for h in range(4):
                    mm(pb[bv_][:, h * 128:(h + 1) * 128], wT[:, h, :], S_bf[:, h, :], True, True, ["wT", "S_bf"], [pk(bv_)])
                tt("dve", vnew[:].rearrange("p h i -> p (h i)"), u_sb[:].rearrange("p h i -> p (h i)"), pb[bv_][:], ALU.subtract,
                   ["u_sb", pk(bv_)], ["vnew"])
                pbp.put(bv_)
                for h in range(4):
                    mm(pb[bo][:, h * 128:(h + 1) * 128], qdT[:, h, :], S_bf[:, h, :], True, False, ["qdT", "S_bf"], [pk(bo)])
                    mm(pb[bo][:, h * 128:(h + 1) * 128], qkT[:, h, :], vnew[:, h, :], False, True, ["qkT", "vnew"], [pk(bo)])
                bs = pbp.get()
                for h in range(4):
                    mm(pb[bs][:, h * 128:(h + 1) * 128], kdec[:, h, :], vnew[:, h, :], True, True, ["kdec", "vnew"], [pk(bs)])
                tt("dve", S_f[:], S_f[:], cf[:, 6, t4].unsqueeze(2).to_broadcast([128, 4, 128]), ALU.mult, ["S_f", "cf"], ["S_f"])
                tt("dve", S_f[:].rearrange("p h i -> p (h i)"), S_f[:].rearrange("p h i -> p (h i)"), pb[bs][:], ALU.add, ["S_f", pk(bs)], ["S_f"])
                pbp.put(bs)
                cp("act", S_bf[:], S_f[:], ["S_f"], ["S_bf"])
            else:
                memset("pool", wTx[:], 0.0, ["wTx", "convf"])
                memset("pool", qdTx[:], 0.0, ["qdTx"])
                tt("dve", GB[:], Bsel[:].unsqueeze(2).to_broadcast([128, 16, 4]), G.unsqueeze(1).to_broadcast([128, 16, 4]), ALU.mult,
                   ["Bsel", "cf"], ["GB"])
                b = pbp.get()
                mm(pb[b][:, 0:64], ones_f[:], GB[:].rearrange("p s h -> p (s h)"), True, True, ["ones_f", "GB"], [pk(b)])
                act(egts[:].rearrange("p s h -> p (s h)"), pb[b][:, 0:64], AF.Exp, [pk(b)], ["egts"])
                pbp.put(b)
                wx3 = lambda tns: tns[:].rearrange("p (s r) -> p s r", r=136)[:, 0:16, 0:8]
                for h in range(4):
                    dma("sp", S0f[:], st_S[:, h, :, :].rearrange("s k v -> k s v"), [], ["S0f"], "S0f")
                    cp("dve", S0b[:, 0:8, :], S0f[:, 0:8, :], ["S0f"], ["S0b"])
                    cp("act", S0b[:, 8:16, :], S0f[:, 8:16, :], ["S0f"], ["S0b"])
                    cp("pool", wx3(wTx), wT[:, h, :].rearrange("p (s c) -> p s c", c=8), ["wT"], ["wTx"])
                    cp("pool", wx3(qdTx), qdT[:, h, :].rearrange("p (s c) -> p s c", c=8), ["qdT"], ["qdTx"])
                    bv_ = pbp.get()
                    for s_ in range(16):
                        mm(pb[bv_][:, 0:128], wTx[:, s_ * 128:(s_ + 1) * 128], S0b[:, s_, :], s_ == 0, s_ == 15, ["wTx", "S0b"], [pk(bv_)])
                    tt("dve", vnew[:, h, :], u_sb[:, h, :], pb[bv_][:, 0:128], ALU.subtract, ["u_sb", pk(bv_)], ["vnew"])
                    pbp.put(bv_)
                    for s_ in range(16):
                        mm(pb[bo][:, h * 128:(h + 1) * 128], qdTx[:, s_ * 128:(s_ + 1) * 128], S0b[:, s_, :], s_ == 0, False, ["qdTx", "S0b"], [pk(bo)])
                    mm(pb[bo][:, h * 128:(h + 1) * 128], qkT[:, h, :], vnew[:, h, :], False, True, ["qkT", "vnew"], [pk(bo)])
                    for g4 in range(4):
                        if g4 % 2 == 0:
                            tt("pool", kdx[:], kdec[:, h, :].unsqueeze(1).to_broadcast([128, 8, 128]),
                               Bsel[:, g4 * 4:g4 * 4 + 8].unsqueeze(2).to_broadcast([128, 8, 128]), ALU.mult, ["kdec", "Bsel"], ["kdx"])
                        bs = pbp.get()
                        for s4 in range(4):
                            s_ = g4 * 4 + s4
                            mm(pb[bs][:, s4 * 128:(s4 + 1) * 128], kdx[:, s_ % 8, :], vnew[:, h, :], True, True, ["kdx", "vnew"], [pk(bs)])
                        for s4 in range(4):
                            s_ = g4 * 4 + s4
                            stt("dve", Sout[:, s_, :], S0f[:, s_, :], egts[:, s_, h:h + 1], pb[bs][:, s4 * 128:(s4 + 1) * 128], ALU.mult, ALU.add,
                                ["S0f", "egts", pk(bs)], ["Sout"])
                        pbp.put(bs)
                    dma("sp", o_S_s[:, h, :, :].rearrange("s k v -> k s v"), Sout[:], ["Sout"], [], "Sout")
                P.op("dve", lambda e: e.tensor_copy(out=small[:, 62:63], in_=small[:, 62:63]),
                     reads=["wTx", "qdTx"], writes=["S0f", "S0b", "Sout", "kdx", "actT", "convf"])
            cp("act", o_sb[:].rearrange("p h i -> p (h i)"), pb[bo][:], [pk(bo)], ["o_sb"])
            pbp.put(bo)
            tt("pool", u_sb[:], o_sb[:], o_sb[:], ALU.mult, ["o_sb", "u_sb"], ["u_sb"])
            P.op("dve", lambda e: e.tensor_reduce(out=coef[:, 28:32], in_=u_sb[:], axis=AX.X, op=ALU.add), reads=["u_sb"], writes=["coef"])
            ts("dve", coef[:, 28:32], coef[:, 28:32], 1.0 / 128, EPS, ALU.mult, ALU.add, ["coef"], ["coef"])
            pw(coef[:, 32:36], coef[:, 28:32], 4, ["coef"], ["coef"])
            tt("dve", on_bf[:], o_sb[:], coef[:, 32:36].unsqueeze(2).to_broadcast([128, 4, 128]), ALU.mult, ["o_sb", "coef"], ["on_bf"])

            def _otail(tc_=tc_):
                b = pbp.get()
                for h in range(4):
                    tr(pbf(b)[:, h * 128:(h + 1) * 128], on_bf[:, h, :], ident_b[:], ["on_bf", "ident_b"], [pk(b)])
                stt("dve", mixT[:, 4:8, tc_], pbf(b)[:, 0:512].rearrange("p (h i) -> p h i", i=128), dngT, szT[:, :, tc_], ALU.mult, ALU.mult,
                    [pk(b), "colsT", "szT"], [("mixT", tc_.start // 128)])
                pbp.put(b)
            otail.append(_otail)
        if last_p:
            dma("sp", o_S_p.rearrange("h k v -> k h v"), S_f[:], ["S_f"], [], "S_f_out")

        while otail:
            otail.pop(0)()
        while fdone[0] < ntile:
            f_tile(fdone[0])
        for s_ in sF:
            wrp.put(s_)

        pfc = []

        def fconv(i, ui, chunks):
            uk = ("ub", ui)
            for a in range(2):
                bg = pbp.get(); bv_ = pbp.get()
                for (bb, q4) in ((bg, a), (bv_, 2 + a)):
                    for j in range(FK):
                        src = conv_view(ubs[ui][:, q4] if samp else ubp[ui][:, q4], j)
                        mm(outv(pb[bb][:, 0:NT]), dgf[ui][:, q4, j, :], src, j == 0, j == FK - 1,
                           [("dgf", ui), (uk, "h"), (uk, q4)], [pk(bb)])
                sg = sig[a]; sgk = ("sig", a)
                cg = chunks[a]; cv = chunks[2 + a]
                act(sg[:, 0:NT], pb[bg][:, 0:NT], AF.Silu, [pk(bg), "colsT"], [sgk], bias=fbT[:, cg:cg + 1])
                pbp.put(bg)
                stt("dve", actT[:, 2 * i + a, 0:NT], pb[bv_][:, 0:NT], fbT[:, cv:cv + 1], sg[:, 0:NT], ALU.add, ALU.mult,
                    [pk(bv_), "colsT", sgk], ["actT"])
                pbp.put(bv_)
            if not samp:
                for q4 in range(4):
                    cp("pool", uhist[:, chunks[q4], :], ubp[ui][:, q4, L:L + 2], [(uk, q4)], ["uhist"])

        for i in range(11):
            ui = i % 2
            uk = ("ub", ui)
            chunks = [2 * i, 2 * i + 1, 22 + 2 * i, 22 + 2 * i + 1]
            if samp:
                for q4 in range(4):
                    cp("pool", ubs[ui][:, q4, :, 0:2], uhist_s[:, chunks[q4], :, :], ["uhist_s"], [(uk, "h")])
            else:
                for q4 in range(4):
                    cp("pool", ubp[ui][:, q4, 0:2], uhist[:, chunks[q4], :], ["uhist"], [(uk, "h")])
            s = wload()
            if not samp and (hgroups or hpend):
                hist_step()
            for q4 in range(4):
                tt("pool", dgf[ui][:, q4, :, :], ident_b[:].unsqueeze(1).to_broadcast([128, 3, 128]),
                   fw3[:, :, chunks[q4]].unsqueeze(2).to_broadcast([128, 3, 128]), ALU.mult, ["ident_b", "fwT"], [("dgf", ui)])
            for q4 in range(4):
                b = pbp.get()
                for kc in range(8):
                    mm(pb[b][:, 0:NT], wr[s][:, kc, q4 * 128:(q4 + 1) * 128], hT[:, kc, 0:NT], kc == 0, kc == 7, [("wr", s), "hT"], [pk(b)])
                if samp:
                    cp(evac_eng(), ubs[ui][:, q4, :, 2:10], outv(pb[b][:, 0:NT]), [pk(b)], [(uk, q4)])
                    cp("dve", upf_s[:, chunks[q4], :].rearrange("p (s j) -> p s j", j=2), outv(pb[b][:, 0:NT])[:, :, 6:8], [pk(b)], ["upf_s"])
                else:
                    cp(evac_eng(), ubp[ui][:, q4, 2:2 + L], pb[b][:, 0:NT], [pk(b)], [(uk, q4)])
                    if last_p:
                        cp("dve", upf_p[:, chunks[q4], :], pb[b][:, NT - 4:NT], [pk(b)], ["upf_p"])
                pbp.put(b)
                if q4 == 1 and pfc:
                    pfc.pop(0)()
            wrp.put(s)
            pfc.append(lambda i=i, ui=ui, chunks=chunks: fconv(i, ui, chunks))
        while hpend:
            _hg_tr(*hpend.pop(0))
        while pfc:
            pfc.pop(0)()

        for half in range(2):
            accs = [pbp.get() for _ in range(ntile)]
            for gi, (k0, nk) in enumerate(((0, 8), (8, 8), (16, 6))):
                s = wload([(w_down[k0 * 128:(k0 + nk) * 128, half * 512:(half + 1) * 512], nk, 0, 512)])
                for t in range(ntile):
                    for kk in range(nk):
                        kc = k0 + kk
                        mm(pb[accs[t]][:], actT[:, kc, t * 128:(t + 1) * 128], wr[s][:, kk, :], kc == 0, kc == 21,
                           ["actT", ("wr", s)], [pk(accs[t])])
                wrp.put(s)
            for t in range(ntile):
                tt("dve", xt[:, t, half * 512:(half + 1) * 512], xt[:, t, half * 512:(half + 1) * 512], pb[accs[t]][:], ALU.add,
                   [("xt", t), pk(accs[t])], [("xt", t)])
                pbp.put(accs[t])

        for t in range(ntile):
            norm_T(t, None, hT, "hT", do_norm=False)
            ptile = ptile4[:, t, :]
            b = pbp.get()
            for k2 in range(2):
                tr(pbf(b)[:, k2 * 128:(k2 + 1) * 128], ptile[:, k2 * 128:(k2 + 1) * 128], ident_b[:], [("ptile", t), "ident_b"], [pk(b)])
            cp("act", pT[:, :, t * 128:(t + 1) * 128], pbf(b)[:, 0:256].rearrange("p (k c) -> p k c", c=128), [pk(b)], ["pT"])
            pbp.put(b)
        sJ = [wload(), wload()]
        ydst = y_s if samp else y_p[blk * 512:(blk + 1) * 512, :]
        for t in range(ntile):
            if blk < 3 and t >= 2:
                norm_front(t - 2)
            for half in range(2):
                bg = pbp.get(); be = pbp.get()
                for kc in range(8):
                    mm(pb[bg][:], hT[:, kc, t * 128:(t + 1) * 128], wr[sJ[half]][:, kc, :], kc == 0, kc == 7,
                       [("hTj", t), ("wr", sJ[half])], [pk(bg)])
                for k2 in range(2):
                    mm(pb[be][:], pT[:, k2, t * 128:(t + 1) * 128], wple_sb[:, k2, half * 512:(half + 1) * 512], k2 == 0, k2 == 1,
                       ["pT", "wple_sb"], [pk(be)])
                sg = sig[half]; sgk = ("sig", half)
                act(sg[:], pb[bg][:], AF.Sigmoid, [pk(bg)], [sgk])
                pbp.put(bg)
                tt("dve", sg[:], sg[:], pb[be][:], ALU.mult, [sgk, pk(be)], [sgk])
                pbp.put(be)
                tt("dve", xt[:, t, half * 512:(half + 1) * 512], xt[:, t, half * 512:(half + 1) * 512], sg[:], ALU.add,
                   [("xt", t), sgk], [("xt", t)])
            if blk < 3 and t >= 2:
                norm_back(t - 2, gmixT, hT, "hT")
            xk = ("xt", t)
            sm = small[:, 24 + t * 4:24 + t * 4 + 4]; smk = ("smallf", t)
            act(junk[:], xt[:, t, :], AF.Square, [xk], [("sqt", 0), smk], accum=sm[:, 0:1])
            ts("dve", sm[:, 1:2], sm[:, 0:1], 1.0 / D, EPS, ALU.mult, ALU.add, [smk], [smk])
            pw(sm[:, 3:4], sm[:, 1:2], 1, [smk], [smk])
            stt("dve", xt[:, t, :], xt[:, t, :], sm[:, 3:4], gfin_bc[:], ALU.mult, ALU.mult, [xk, smk, "gfin_bc"], [xk])
            dma("sp", ydst[t * 128:(t + 1) * 128, :], xt[:, t, :], [xk], [], ("yo", t))
            if blk < 3:
                nsrc = x_p[(blk + 1) * 512:(blk + 2) * 512, :]
                dma("sp", xt[:, t, :], nsrc[t * 128:(t + 1) * 128, :], [], [("xt", t)], ("xt", t))
        if blk < 3:
            norm_T(ntile - 2, gmixT, hT, "hT")
            norm_T(ntile - 1, gmixT, hT, "hT")
            pre_a[0] = True
        for s_ in sJ:
            wrp.put(s_)

        oc = [0]

        def tr_out(src_ap_fn, nch, ncol, dst_rows_fn):
            for g0 in range(0, nch, 4):
                b = pbp.get()
                for c in range(g0, min(g0 + 4, nch)):
                    tr(pb[b][0:ncol, (c - g0) * 128:(c - g0 + 1) * 128], src_ap_fn(c), ident_f[:], ["ident_f", "stsrc"], [pk(b)])
                n = min(4, nch - g0) * 128
                oi = oc[0] % 2; oc[0] += 1
                ob = sig[oi]; okey = ("sig", oi); dk = ("orow", oi)
                cp("dve", ob[0:ncol, 0:n], pb[b][0:ncol, 0:n], [pk(b)], [okey])
                pbp.put(b)
                dst_rows_fn(g0, n, ob, okey, dk)

        if last_p:
            P.op("dve", lambda e: e.tensor_copy(out=small[:, 63:64], in_=small[:, 63:64]), reads=["gluf_p", "qkvf_p", "upf_p"], writes=["stsrc"])
            tr_out(lambda c: gluf_p[:, c, :], 4, 32, lambda g0, n, ob, okey, dk: dma("sp", o_conf_p[:, :], ob[2:32, 0:512], [okey], [], dk))
            tr_out(lambda c: qkvf_p[:, c, :], 12, 4,
                   lambda g0, n, ob, okey, dk: dma("sp", o_dn_p[:, g0 * 128:g0 * 128 + n], ob[1:4, 0:n], [okey], [], dk))
            tr_out(lambda c: upf_p[:, c, :], 44, 4,
                   lambda g0, n, ob, okey, dk: dma("sp", o_ffn_p[:, g0 * 128:g0 * 128 + n], ob[2:4, 0:n], [okey], [], dk))
        if samp:
            P.op("dve", lambda e: e.tensor_copy(out=small[:, 63:64], in_=small[:, 63:64]), reads=["gluf_s", "qkvf_s", "upf_s"], writes=["stsrc"])

            def conf_s_out(g0, n, ob, okey, dk):
                for s_ in range(16):
                    dma("sp", o_conf_s[s_, 22:30, :], ob[s_ * 8:(s_ + 1) * 8, 0:512], [okey], [], dk)
            tr_out(lambda c: gluf_s[:, c, :], 4, 128, conf_s_out)
            dma("sp", o_conf_s[:, 0:22, :], st_conf.rearrange("(s j) c -> s j c", j=30)[:, 8:30, :], [], [], "d2d")

            tr_out(lambda c: qkvf_s[:, c, :, :].rearrange("p s j -> p (s j)"), 12, 48,
                   lambda g0, n, ob, okey, dk: dma("sp", o_dn_s.rearrange("s j c -> (s j) c")[:, g0 * 128:g0 * 128 + n], ob[0:48, 0:n], [okey], [], dk))
            tr_out(lambda c: upf_s[:, c, :], 44, 32,
                   lambda g0, n, ob, okey, dk: dma("sp", o_ffn_s[:, g0 * 128:g0 * 128 + n], ob[0:32, 0:n], [okey], [], dk))

    for blk in range(5):
        if blk == 4:
            while hgroups or hpend:
                hist_step()
            P.barrier()
        import os
        if os.environ.get("MK_DBG"):
            print("block", blk, "starts at op", len(P.ops))
        do_block(blk)

    P.emit()
    st.close()
    return nc, P.stats


_CACHE = {}


def kernel(x_prompt, x_sample, p_prompt, p_sample, state_conf_buf, state_dn_conv_buf, state_dn_S, state_ffn_buf,
           norm_mix_g, w_in, conf_dw_w, conf_dw_b, conf_ln_g, conf_ln_b, dn_conv_w, dn_a_log, dn_dt_bias,
           dn_norm_g, w_out, norm_ffn_g, w_up, ffn_conv_w, ffn_conv_b, w_down, w_ple, w_ple_gate, norm_final_g):
    f = lambda a: np.ascontiguousarray(np.asarray(a, dtype=np.float32))
    if "nc" not in _CACHE:
        _CACHE["nc"] = build_program()[0]
    nc = _CACHE["nc"]
    shared = dict(
        g_mix=f(norm_mix_g).reshape(8, 128), w_in=f(w_in)[0], cw_w=f(conf_dw_w).reshape(CK * 4, 128),
        cw_b=f(conf_dw_b).reshape(4, 128), ln_g=f(conf_ln_g).reshape(4, 128), ln_b=f(conf_ln_b).reshape(4, 128),
        dn_w=f(dn_conv_w).reshape(DK * 12, 128), a_log=f(dn_a_log).reshape(1, 4), dt_b=f(dn_dt_bias).reshape(1, 4),
        dn_g=f(dn_norm_g).reshape(1, 128), w_out=f(w_out)[0], g_ffn=f(norm_ffn_g).reshape(8, 128), w_up=f(w_up)[0],
        f_w=f(ffn_conv_w).reshape(FK * 44, 128), f_b=f(ffn_conv_b).reshape(44, 128), w_down=f(w_down)[0],
        w_ple=f(w_ple)[0], w_pg=f(w_ple_gate)[0], g_fin=f(norm_final_g).reshape(1, D))
    xp = f(x_prompt); xs = f(x_sample); pp = f(p_prompt)[0]; ps_ = f(p_sample)[0]
    sc = f(state_conf_buf)[0]; sd = f(state_dn_conv_buf)[0]; sS = f(state_dn_S)[0]; sf = f(state_ffn_buf)[0]
    in_maps = []
    for c in range(NCORE):
        sl = slice(c * DEC_PER, (c + 1) * DEC_PER)
        m = dict(shared)
        m.update(x_p=xp[c], x_s=xs[sl].reshape(128, D), p_p=pp[c], p_s=ps_[sl].reshape(128, 256),
                 st_conf=sc[sl].reshape(DEC_PER * 30, CW), st_dn=sd[sl].reshape(DEC_PER * 3, 3 * DNW),
                 st_S=np.ascontiguousarray(sS[sl]), st_ffn=sf[sl].reshape(DEC_PER * 2, 2 * DFF))
        in_maps.append(m)
    res = run_bass_kernel_spmd(nc, in_maps, core_ids=list(range(NCORE)))
    R = res.results
    cat = lambda k: np.stack([np.asarray(R[c][k], dtype=np.float32) for c in range(NCORE)], axis=0)
    y_prompt = cat("y_p")
    y_sample = cat("y_s").reshape(128, DEC_SEQ, D)
    pc = cat("o_conf_p")[None]
    pd = cat("o_dn_p")[None]
    ps = cat("o_S_p")[None]
    pf = cat("o_ffn_p")[None]
    sc_o = cat("o_conf_s").reshape(1, 128, 30, CW)
    sd_o = cat("o_dn_s").reshape(1, 128, 3, 3 * DNW)
    ss_o = cat("o_S_s").reshape(1, 128, 4, 128, 128)
    sf_o = cat("o_ffn_s").reshape(1, 128, 2, 2 * DFF)
    return (y_prompt, y_sample, pc, pd, ps, pf, sc_o, sd_o, ss_o, sf_o)
```

```python
import contextlib
import numpy as np
import concourse.bass as bass
import concourse.mybir as mybir
from concourse.bass_utils import run_bass_kernel_spmd

F32 = mybir.dt.float32
BF16 = mybir.dt.bfloat16
AF = mybir.ActivationFunctionType
ALU = mybir.AluOpType
AX = mybir.AxisListType

D = 1024
SEQ = 2048
NCORE = 8
DEC_PER = 16
DEC_SEQ = 8
CW = 512
CK = 31
DNW = 512
DK = 4
INC = 3080
DFF = 2816
FK = 3
EPS = 1e-6
NEG = -30000.0


class Prog:
    ENG = ("pe", "act", "dve", "pool", "sp")

    def __init__(self, nc):
        self.nc = nc
        self.ops = []
        self.last_w = {}
        self.readers = {}
        import os
        self.limit = int(os.environ.get("MK_LIMIT", "100000000"))

    def op(self, eng, fn, reads=(), writes=(), dma=None):
        if len(self.ops) >= self.limit:
            return -1
        i = len(self.ops)
        writes = list(writes) + [r for r in reads if isinstance(r, tuple) and r[0] == "pb" and r not in writes]
        deps = set()
        for r in reads:
            if r in self.last_w:
                deps.add(self.last_w[r])
        for w in writes:
            if w in self.last_w:
                deps.add(self.last_w[w])
            for rd in self.readers.get(w, ()):
                deps.add(rd)
        for r in reads:
            lst = self.readers.setdefault(r, [])
            if dma is None:
                lst[:] = [j for j in lst if self.ops[j]["dma"] is not None or self.ops[j]["eng"] != eng]
            lst.append(i)
        for w in writes:
            self.last_w[w] = i
            self.readers[w] = []
        deps.discard(i)
        if dma is not None:
            deps = {d for d in deps if self.ops[d]["dma"] != dma}
        best = {}
        keep = set()
        for d in deps:
            od = self.ops[d]
            if od["dma"] is not None:
                keep.add(d)
            else:
                if od["eng"] not in best or best[od["eng"]] < d:
                    best[od["eng"]] = d
        keep.update(best.values())
        import sys as _sys
        f = _sys._getframe(1)
        while f.f_code.co_name not in ("do_block", "build_program", "norm_T", "load_cols", "wload", "tr_out") and f.f_back is not None:
            f = f.f_back
        self.ops.append(dict(eng=eng, fn=fn, deps=keep, dma=dma, line=f.f_lineno))
        return i

    def barrier(self):
        last = {}
        for i, o in enumerate(self.ops):
            k = ("dma", o["dma"]) if o["dma"] is not None else ("eng", o["eng"])
            last[k] = i
        deps = set(last.values())
        if len(self.ops) >= self.limit:
            return
        for e in self.ENG:
            self.ops.append(dict(eng=e, fn=lambda E: E.nop(), deps=set(deps), dma=None))

    def emit(self):
        nc = self.nc
        ops = self.ops
        engs = dict(pe=nc.tensor, act=nc.scalar, dve=nc.vector, pool=nc.gpsimd, sp=nc.sync)
        need = [False] * len(ops)
        for o in ops:
            for d in o["deps"]:
                od = ops[d]
                if od["dma"] is None and (od["eng"] != o["eng"] or o["dma"] is not None or o["eng"] != "pe"):
                    need[d] = True
        cnt = {e: 0 for e in self.ENG}
        opcnt = [0] * len(ops)
        dma_cnt = {}
        dma_after = [0] * len(ops)
        for i, o in enumerate(ops):
            if o["dma"] is not None:
                k = o["dma"]
                dma_cnt[k] = dma_cnt.get(k, 0) + 16
                dma_after[i] = dma_cnt[k]
            elif need[i]:
                cnt[o["eng"]] += 1
                opcnt[i] = cnt[o["eng"]]
        with contextlib.ExitStack() as st:
            sem = {e: st.enter_context(nc.semaphore("s_" + e)) for e in self.ENG}
            dsem = {k: st.enter_context(nc.semaphore("d_%d" % n)) for n, k in enumerate(dma_cnt)}
            waited = {e: {} for e in self.ENG}
            issued = {}
            nw = 0
            for i, o in enumerate(ops):
                e = o["eng"]
                E = engs[e]
                for d in sorted(o["deps"]):
                    od = ops[d]
                    if od["dma"] is not None:
                        key = ("d", od["dma"]); val = issued.get(od["dma"], 0); s = dsem[od["dma"]]
                    else:
                        if od["eng"] == e and o["dma"] is None and e == "pe":
                            continue
                        key = ("e", od["eng"]); val = opcnt[d]; s = sem[od["eng"]]
                    if waited[e].get(key, 0) >= val:
                        continue
                    waited[e][key] = val
                    E.wait_ge(s, val)
                    nw += 1
                ins = o["fn"](E)
                try:
                    o["iname"] = str(ins.ins.name)
                except Exception:
                    o["iname"] = None
                if o["dma"] is not None:
                    issued[o["dma"]] = dma_after[i]
                    ins.then_inc(dsem[o["dma"]], 16)
                elif need[i]:
                    ins.then_inc(sem[e], 1)
            for k, v in dma_cnt.items():
                nc.sync.wait_ge(dsem[k], v)
            self.stats = dict(n_ops=len(ops), n_waits=nw, cnt=cnt, ndsem=len(dma_cnt))


class Pool8:
    def __init__(self, n):
        self.free = list(range(n))

    def get(self):
        return self.free.pop(0)

    def put(self, i):
        self.free.append(i)


def build_program():
    nc = bass.Bass("TRN2", target_bir_lowering=False)
    st = contextlib.ExitStack()
    P = Prog(nc)

    def din(name, shape):
        return nc.dram_tensor(name, list(shape), F32, kind="ExternalInput").ap()

    def dout(name, shape):
        return nc.dram_tensor(name, list(shape), F32, kind="ExternalOutput").ap()

    x_p = din("x_p", [SEQ, D]); x_s = din("x_s", [128, D])
    p_p = din("p_p", [SEQ, 256]); p_s = din("p_s", [128, 256])
    st_conf = din("st_conf", [DEC_PER * 30, CW]); st_dn = din("st_dn", [DEC_PER * 3, 3 * DNW])
    st_S = din("st_S", [DEC_PER, 4, 128, 128]); st_ffn = din("st_ffn", [DEC_PER * 2, 2 * DFF])
    g_mix = din("g_mix", [8, 128]); w_in = din("w_in", [D, INC]); cw_w = din("cw_w", [CK * 4, 128])
    cw_b = din("cw_b", [4, 128]); ln_g = din("ln_g", [4, 128]); ln_b = din("ln_b", [4, 128])
    dn_w = din("dn_w", [DK * 12, 128]); a_log = din("a_log", [1, 4]); dt_b = din("dt_b", [1, 4])
    dn_g = din("dn_g", [1, 128]); w_out = din("w_out", [D, D]); g_ffn = din("g_ffn", [8, 128])
    w_up = din("w_up", [D, 2 * DFF]); f_w = din("f_w", [FK * 44, 128]); f_b = din("f_b", [44, 128])
    w_down = din("w_down", [DFF, D]); w_ple = din("w_ple", [256, D]); w_pg = din("w_pg", [D, D])
    g_fin = din("g_fin", [1, D])

    y_p = dout("y_p", [SEQ, D]); y_s = dout("y_s", [128, D])
    o_conf_p = dout("o_conf_p", [30, CW]); o_dn_p = dout("o_dn_p", [3, 3 * DNW])
    o_S_p = dout("o_S_p", [4, 128, 128]); o_ffn_p = dout("o_ffn_p", [2, 2 * DFF])
    o_conf_s = dout("o_conf_s", [DEC_PER, 30, CW]); o_dn_s = dout("o_dn_s", [DEC_PER, 3, 3 * DNW])
    o_S_s = dout("o_S_s", [DEC_PER, 4, 128, 128]); o_ffn_s = dout("o_ffn_s", [DEC_PER * 2, 2 * DFF])

    def sb(name, shape, dt=F32):
        return st.enter_context(nc.sbuf_tensor(name, list(shape), dt))

    pb = [st.enter_context(nc.psum_tensor("pb%d" % i, [128, 512], F32)) for i in range(8)]
    pbp = Pool8(8)

    def pk(i):
        return ("pb", i)

    def pbf(i):
        return pb[i][:].bitcast(BF16)

    def mm(out, lhsT, rhs, start, stop, reads, writes):
        P.op("pe", lambda e: e.matmul(out, lhsT=lhsT, rhs=rhs, start=start, stop=stop), reads=reads, writes=writes)

    def tr(out, in_, ident, reads, writes):
        P.op("pe", lambda e: e.transpose(out, in_, ident), reads=reads, writes=writes)

    def act(out, in_, func, reads, writes, bias=None, scale=None, accum=None):
        kw = {}
        if bias is not None:
            kw["bias"] = bias
        if scale is not None:
            kw["scale"] = scale
        if accum is not None:
            kw["accum_out"] = accum
        P.op("act", lambda e: e.activation(out=out, in_=in_, func=func, **kw), reads=reads, writes=writes)

    def tt(eng, out, in0, in1, op, reads, writes):
        P.op(eng, lambda e: e.tensor_tensor(out=out, in0=in0, in1=in1, op=op), reads=reads, writes=writes)

    def stt(eng, out, in0, scalar, in1, op0, op1, reads, writes):
        P.op(eng, lambda e: e.scalar_tensor_tensor(out=out, in0=in0, scalar=scalar, in1=in1, op0=op0, op1=op1),
             reads=reads, writes=writes)

    def ts(eng, out, in0, s1, s2, op0, op1, reads, writes):
        if op1 is None:
            P.op(eng, lambda e: e.tensor_scalar(out=out, in0=in0, scalar1=s1, scalar2=None, op0=op0), reads=reads, writes=writes)
        else:
            P.op(eng, lambda e: e.tensor_scalar(out=out, in0=in0, scalar1=s1, scalar2=s2, op0=op0, op1=op1),
                 reads=reads, writes=writes)

    def cp(eng, out, in_, reads, writes):
        if eng == "act":
            P.op("act", lambda e: e.activation(out=out, in_=in_, func=AF.Identity), reads=reads, writes=writes)
        else:
            P.op(eng, lambda e: e.tensor_copy(out=out, in_=in_), reads=reads, writes=writes)

    def pw(out, in_, n, reads, writes):
        P.op("pool", lambda e: e.tensor_tensor(out=out, in0=in_, in1=mhalf[:].to_broadcast([128, n]), op=ALU.pow),
             reads=list(reads) + ["mhalf"], writes=writes)

    def recip(out, in_, reads, writes):
        P.op("dve", lambda e: e.reciprocal(out=out, in_=in_), reads=reads, writes=writes)

    def dma(q, out, in_, reads, writes, key):
        P.op(q, lambda e: e.dma_start(out=out, in_=in_), reads=reads, writes=writes, dma=key)

    def memset(eng, ap, val, writes):
        P.op(eng, lambda e: e.memset(ap, val), writes=writes)

    def asel(out, in_, pattern, op, fill, base, cm, reads, writes):
        P.op("pool", lambda e: e.affine_select(out=out, in_=in_, pattern=pattern, compare_op=op, fill=fill, base=base,
                                               channel_multiplier=cm), reads=reads, writes=writes)

    evq = [0]

    def evac_eng():
        evq[0] += 1
        return "act" if evq[0] % 2 else "dve"

    ident_f = sb("ident_f", [128, 128]); ident_b = sb("ident_b", [128, 128], BF16)
    ones_f = sb("ones_f", [128, 128]); ones_b = sb("ones_b", [128, 128], BF16); odiv_f = sb("odiv_f", [128, 128])
    Ms = [sb("Ms_p", [128, 128]), sb("Ms_s", [128, 128])]
    Um = [sb("U_p", [128, 128]), sb("U_s", [128, 128])]
    NEGT = [sb("NEGT_p", [128, 128], BF16), sb("NEGT_s", [128, 128], BF16)]
    NEGS = [sb("NEGS_p", [128, 128], BF16), sb("NEGS_s", [128, 128], BF16)]
    Bsum = [ones_f, sb("Bsum_s", [128, 128])]
    Bsel = sb("Bsel", [128, 16])
    mhalf = sb("mhalf", [128, 1])
    memset("pool", ones_f[:], 1.0, ["ones_f"])
    memset("pool", mhalf[:], -0.5, ["mhalf"])
    memset("pool", odiv_f[:], 1.0 / CW, ["odiv_f"])
    cp("pool", ones_b[:], ones_f[:], ["ones_f"], ["ones_b"])
    asel(ident_f[:], ones_f[:], [[-1, 128]], ALU.is_equal, 0.0, 0, 1, ["ones_f"], ["ident_f"])
    cp("pool", ident_b[:], ident_f[:], ["ident_f"], ["ident_b"])
    asel(Ms[0][:], ones_f[:], [[-1, 128]], ALU.is_ge, 0.0, -1, 1, ["ones_f"], ["Ms0"])
    asel(Um[0][:], ones_f[:], [[1, 128]], ALU.is_ge, 0.0, 0, -1, ["ones_f"], ["U0"])
    v3 = lambda t: t[:].rearrange("p (a b) -> p a b", b=8)
    asel(v3(Bsum[1]), v3(ones_f), [[-8, 16], [0, 8]], ALU.is_ge, 0.0, 0, 1, ["ones_f"], ["Bsum1"])
    asel(v3(Bsum[1]), v3(Bsum[1]), [[8, 16], [0, 8]], ALU.is_ge, 0.0, 7, -1, ["Bsum1"], ["Bsum1"])
    asel(Bsel[:], ones_f[:, 0:16], [[-8, 16]], ALU.is_ge, 0.0, 0, 1, ["ones_f"], ["Bsel"])
    asel(Bsel[:], Bsel[:], [[8, 16]], ALU.is_ge, 0.0, 7, -1, ["Bsel"], ["Bsel"])
    tt("pool", Ms[1][:], Ms[0][:], Bsum[1][:], ALU.mult, ["Ms0", "Bsum1"], ["Ms1"])
    tt("pool", Um[1][:], Um[0][:], Bsum[1][:], ALU.mult, ["U0", "Bsum1"], ["U1"])
    for v in (0, 1):
        ts("pool", NEGS[v][:], Ms[v][:], -1.0, -NEG, ALU.add, ALU.mult, ["Ms%d" % v], ["NEGS%d" % v])
        ts("pool", NEGT[v][:], Um[v][:], -1.0, -NEG, ALU.add, ALU.mult, ["U%d" % v], ["NEGT%d" % v])

    colsT = sb("colsT", [128, 73])
    cwT = sb("cwT", [128, CK * 4])
    dnwT = sb("dnwT", [128, DK * 12])
    fwT = sb("fwT", [128, FK * 44])
    alog_bc = sb("alog_bc", [128, 4]); dtb_bc = sb("dtb_bc", [128, 4]); negA = sb("negA", [128, 4])
    gfin_bc = sb("gfin_bc", [128, D])
    stage = gfin_bc[:, 0:128]
    wtail = sb("wtail", [128, 8, 8], BF16)
    wple_sb = sb("wple_sb", [128, 2, D], BF16)

    lc_n = [0]

    def load_cols(srcs, dst, dcol):
        r = 0
        kq = lc_n[0]; lc_n[0] += 1
        stage = gfin_bc[:, kq * 128:(kq + 1) * 128]
        for (ap, n) in srcs:
            dma("sp", stage[r:r + n, :], ap, [], ["stage"], ("stage", kq))
            r += n
        b = pbp.get()
        tr(pb[b][:, 0:r], stage[0:r, :], ident_f[0:r, 0:r], ["stage", "ident_f"], [pk(b)])
        cp("dve", dst[:, dcol:dcol + r], pb[b][:, 0:r], [pk(b)], [dst.name])
        pbp.put(b)

    load_cols([(g_mix, 8), (g_ffn, 8), (cw_b, 4), (ln_g, 4), (ln_b, 4), (f_b, 44), (dn_g, 1)], colsT, 0)
    load_cols([(cw_w, 124)], cwT, 0)
    load_cols([(dn_w, 48)], dnwT, 0)
    load_cols([(f_w[0:128, :], 128)], fwT, 0)
    load_cols([(f_w[128:132, :], 4)], fwT, 128)
    dma("sp", alog_bc[:], a_log.to_broadcast([128, 4]), [], ["alog_bc"], "alog_bc")
    dma("sp", dtb_bc[:], dt_b.to_broadcast([128, 4]), [], ["dtb_bc"], "dtb_bc")
    dma("sp", gfin_bc[:], g_fin.to_broadcast([128, D]), [], ["gfin_bc", "stage"], "gfin_bc")
    dma("pool", wtail[:], w_in[:, 3072:3080].rearrange("(k p) c -> p k c", p=128), [], ["wtail"], "wtail")
    dma("pool", wple_sb[:], w_ple.rearrange("(k p) c -> p k c", p=128), [], ["wple_sb"], "wple_sb")
    act(negA[:], alog_bc[:], AF.Exp, ["alog_bc"], ["negA"])
    ts("dve", negA[:], negA[:], -1.0, None, ALU.mult, None, ["negA"], ["negA"])
    gmixT = colsT[:, 0:8]; gffnT = colsT[:, 8:16]
    cwbT = colsT[:, 16:20]; lngT = colsT[:, 20:24]; lnbT = colsT[:, 24:28]; fbT = colsT[:, 28:72]; dngT = colsT[:, 72:73]
    cw3 = cwT[:].rearrange("p (j k) -> p j k", k=4)
    dnw3 = dnwT[:].rearrange("p (j k) -> p j k", k=12)
    fw3 = fwT[:].rearrange("p (j k) -> p j k", k=44)

    xt = sb("xt", [128, 4, D])
    xrest = xt[:, 1:4, :].rearrange("p a b -> p (a b)")
    hT = sb("hT", [128, 8, 512], BF16)
    mixT = sb("mixT", [128, 8, 512], BF16)
    hbs = [sb("hb%d" % i, [128, D], BF16) for i in range(2)]
    hb = hbs[0]
    cbp = sb("cbp", [128, 4, 30 + 512], BF16)
    cbs = xrest[:, 1408:1408 + 1216].bitcast(BF16).rearrange("p (k s j) -> p k s j", k=4, s=16)
    uniB = sb("uniB", [128, 4 * 512])
    convf = uniB[:].rearrange("p (k n) -> p k n", n=512)
    sqt = [sb("sqt%d" % i, [128, 512]) for i in range(2)]
    junk = sqt[0][:].bitcast(BF16)
    gluf_p = sb("gluf_p", [128, 4, 32]); gluf_s = sb("gluf_s", [128, 4, 128])
    sig = [sb("sig%d" % i, [128, 512]) for i in range(2)]
    qbp = [sb("qbp%d" % i, [128, 3 + 512], BF16) for i in range(2)]
    qbs = [sb("qbs%d" % i, [128, 16, 11], BF16) for i in range(2)]
    qhist = sb("qhist", [128, 12, 3], BF16)
    qhist_s = sb("qhist_s", [128, 12, 16, 3], BF16)
    qkvf_p = sb("qkvf_p", [128, 12, 4]); qkvf_s = sb("qkvf_s", [128, 12, 16, 3])
    qkvT = sb("qkvT", [128, 12, 512], BF16)
    szT = sb("szT", [128, 4, 512], BF16)
    ba = sb("ba", [128, 4, 8])
    uniC = sb("uniC", [128, 2 * 4 * 514], BF16)
    ubp = [uniC[:, i * 2056:(i + 1) * 2056].rearrange("p (q n) -> p q n", n=514) for i in range(2)]
    ubs = [sb("ubs%d" % i, [128, 4, 16, 10], BF16) for i in range(2)]
    uhist = sb("uhist", [128, 44, 2], BF16)
    uhist_s = sb("uhist_s", [128, 44, 16, 2], BF16)
    upf_p = sb("upf_p", [128, 44, 4]); upf_s = xrest[:, 0:1408].rearrange("p (c n) -> p c n", n=32)
    uniA = sb("uniA", [128, 22 * 512], BF16)
    actT = uniA[:].rearrange("p (k n) -> p k n", n=512)
    NW = 3
    wr = [sb("wr%d" % i, [128, 8, 512], BF16) for i in range(NW)]
    wrp = Pool8(NW)
    dgc = [sb("dgc%d" % i, [128, 8, 128], BF16) for i in range(2)]
    dgq = [sb("dgq%d" % i, [128, 4, 128], BF16) for i in range(2)]
    dgf = [sb("dgf%d" % i, [128, 4, 3, 128], BF16) for i in range(2)]
    small = sb("small", [128, 64])
    ptile4 = sb("ptile4", [128, 4, 256], BF16); pT = sb("pT", [128, 2, 512], BF16)
    tmqk = sb("tmqk", [128, 8, 128], BF16); tmv = sb("tmv", [128, 4, 128], BF16)
    gU = sb("gU", [128, 4, 128]); ETm = sb("ETm", [128, 4, 128]); Em = sb("Em", [128, 4, 128])
    A_bf = sb("A_bf", [128, 4, 128], BF16); qkT = sb("qkT", [128, 4, 128], BF16)
    Pm = [sb("Pm%d" % i, [128, 4, 128], BF16) for i in range(2)]
    PTm = [sb("PTm%d" % i, [128, 4, 128], BF16) for i in range(2)]
    Rm = [sb("Rm%d" % i, [128, 4, 128], BF16) for i in range(2)]
    PT3 = sb("PT3", [128, 4, 128], BF16)
    vb = sb("vb", [128, 4, 128], BF16); kbe = sb("kbe", [128, 4, 128], BF16); kdec = sb("kdec", [128, 4, 128], BF16)
    qdec = sb("qdec", [128, 4, 128], BF16); qdT = sb("qdT", [128, 4, 128], BF16); wT = sb("wT", [128, 4, 128], BF16)
    u_sb = sb("u_sb", [128, 4, 128]); vnew = sb("vnew", [128, 4, 128], BF16); o_sb = sb("o_sb", [128, 4, 128])
    on_bf = sb("on_bf", [128, 4, 128], BF16)
    S_f = sb("S_f", [128, 4, 128]); S_bf = sb("S_bf", [128, 4, 128], BF16)
    coef = sb("coef", [128, 40])
    cf = sb("cf", [128, 9, 16])
    S0f = uniA[:, 0:4096].bitcast(F32).rearrange("p (s v) -> p s v", v=128)
    Sout = uniA[:, 4096:8192].bitcast(F32).rearrange("p (s v) -> p s v", v=128)
    S0b = uniA[:, 8192:10240].rearrange("p (s v) -> p s v", v=128)
    kdx = uniA[:, 10240:11264].rearrange("p (s v) -> p s v", v=128)
    wTx = uniB[:, 0:1088].bitcast(BF16)
    qdTx = uniC[:, 0:2176]
    _upf_flat = upf_p[:].rearrange("p c n -> p (c n)")
    GB = _upf_flat[:, 0:64].rearrange("p (s h) -> p s h", h=4); egts = _upf_flat[:, 64:128].rearrange("p (s h) -> p s h", h=4)
    sttile = sig[0]
    orow = sig[1]

    memset("dve", small[:], 0.0, ["small", "stsrc"] + [("small", t) for t in range(4)] + [("smallf", t) for t in range(4)])
    memset("dve", S_f[:], 0.0, ["S_f"])
    memset("dve", S_bf[:], 0.0, ["S_bf"])
    memset("dve", cbp[:, :, 0:30], 0.0, ["cbp"])
    memset("dve", qhist[:], 0.0, ["qhist"])
    memset("dve", uhist[:], 0.0, ["uhist"])

    def wspecs():
        for blk in range(5):
            for half in range(2):
                yield [(w_in[:, half * 256:(half + 1) * 256], 8, 0, 256),
                       (w_in[:, 512 + half * 256:512 + (half + 1) * 256], 8, 256, 256)]
            for wg in range(4):
                yield [(w_in[:, 1024 + wg * 512:1024 + (wg + 1) * 512], 8, 0, 512)]
            for half in range(2):
                yield [(w_out[:, half * 512:(half + 1) * 512], 8, 0, 512)]
            for i in range(11):
                yield [(w_up[:, i * 256:(i + 1) * 256], 8, 0, 256), (w_up[:, DFF + i * 256:DFF + (i + 1) * 256], 8, 256, 256)]
            for half in range(2):
                for (k0, nk) in ((0, 8), (8, 8), (16, 6)):
                    yield [(w_down[k0 * 128:(k0 + nk) * 128, half * 512:(half + 1) * 512], nk, 0, 512)]
            for half in range(2):
                yield [(w_pg[:, half * 512:(half + 1) * 512], 8, 0, 512)]

    wit = wspecs()
    wq = []

    def wfill():
        while len(wq) + len(wr) - len(wrp.free) - len(wq) < NW and wrp.free:
            parts = next(wit, None)
            if parts is None:
                return
            s = wrp.get()
            for (ap, kc, off, ncols) in parts:
                if len(ap.shape) == 3:
                    dma("pool", wr[s][:, 0:kc, :].rearrange("p k (g c) -> p k g c", g=2), ap.rearrange("(k p) g c -> p k g c", p=128),
                        [], [("wr", s)], ("wr", s))
                else:
                    dma("pool", wr[s][:, 0:kc, off:off + ncols], ap.rearrange("(k p) c -> p k c", p=128), [], [("wr", s)], ("wr", s))
            wq.append(s)

    def wload(parts=None):
        if not wq:
            wfill()
        s = wq.pop(0)
        wfill()
        return s

    def norm_T(t, gT, dst, key_dst, do_norm=True):
        norm_front(t, do_norm)
        norm_back(t, gT, dst, key_dst)

    def norm_front(t, do_norm=True):
        xk = ("xt", t)
        hb = hbs[t % 2]; hk = ("hb", t % 2)
        sm = small[:, t * 5:t * 5 + 5]; smk = ("small", t)
        if do_norm:
            act(junk[:], xt[:, t, :], AF.Square, [xk], [("sqt", 0), smk], accum=sm[:, 0:1])
            ts("dve", sm[:, 2:3], sm[:, 0:1], 1.0 / D, EPS, ALU.mult, ALU.add, [smk], [smk])
            pw(sm[:, 4:5], sm[:, 2:3], 1, [smk], [smk])
            ts("dve", hb[:], xt[:, t, :], sm[:, 4:5], None, ALU.mult, None, [xk, smk], [hk])
        else:
            cp("dve", hb[:], xt[:, t, :], [xk], [hk])

    def norm_back(t, gT, dst, key_dst):
        hb = hbs[t % 2]; hk = ("hb", t % 2)
        b = pbp.get()
        for kc in range(8):
            tr(pbf(b)[:, kc * 128:(kc + 1) * 128], hb[:, kc * 128:(kc + 1) * 128], ident_b[:], [hk, "ident_b"], [pk(b)])
        src = pbf(b).rearrange("p (k c) -> p k c", c=128)
        if gT is not None:
            tt("dve", dst[:, :, t * 128:(t + 1) * 128], src, gT.unsqueeze(2).to_broadcast([128, 8, 128]), ALU.mult,
               [pk(b), "colsT"], [key_dst, ("hTj", t)])
        else:
            cp("act", dst[:, :, t * 128:(t + 1) * 128], src, [pk(b)], [key_dst, ("hTj", t)])
        pbp.put(b)

    pre_a = [False]

    def _hg_dma(kind, ci):
        stg = sqt[1]
        if kind == 0:
            dma("sp", stg[0:48, :], st_dn[:, ci * 512:(ci + 1) * 512], [], [("sqt", 1)], "hstage")
        else:
            dma("sp", stg[0:32, :], st_ffn[:, ci * 512:(ci + 1) * 512], [], [("sqt", 1)], "hstage")

    def _hg_tr(kind, ci):
        stg = sqt[1]
        for k in range(4):
            b = pbp.get()
            if kind == 0:
                tr(pb[b][:, 0:48], stg[0:48, k * 128:(k + 1) * 128], ident_f[0:48, 0:48], [("sqt", 1), "ident_f"], [pk(b)])
                cp("dve", qhist_s[:, ci * 4 + k, :, :], pb[b][:, 0:48].rearrange("p (s j) -> p s j", j=3), [pk(b)], ["qhist_s"])
            else:
                tr(pb[b][:, 0:32], stg[0:32, k * 128:(k + 1) * 128], ident_f[0:32, 0:32], [("sqt", 1), "ident_f"], [pk(b)])
                cp("dve", uhist_s[:, ci * 4 + k, :, :], pb[b][:, 0:32].rearrange("p (s j) -> p s j", j=2), [pk(b)], ["uhist_s"])
            pbp.put(b)

    hgroups = [(0, c3) for c3 in range(3)] + [(1, c11) for c11 in range(11)]
    hpend = []

    def hist_step():
        if hpend:
            _hg_tr(*hpend.pop(0))
        if hgroups:
            g = hgroups.pop(0)
            _hg_dma(*g)
            hpend.append(g)

    def do_block(blk):
        samp = blk == 4
        NT = 128 if samp else 512
        ntile = NT // 128
        v = 1 if samp else 0
        nseq, L = (16, 8) if samp else (1, 512)
        xsrc = x_s if samp else x_p[blk * 512:(blk + 1) * 512, :]
        psrc = p_s if samp else p_p[blk * 512:(blk + 1) * 512, :]
        last_p = blk == 3

        if not pre_a[0]:
            for t in range(ntile):
                dma("sp", xt[:, t, :], xsrc[t * 128:(t + 1) * 128, :], [], [("xt", t)], ("xt", t))
            for t in range(ntile):
                norm_T(t, gmixT, hT, "hT")
        pre_a[0] = False
        for t in range(ntile):
            dma("pool", ptile4[:, t, :], psrc[t * 128:(t + 1) * 128, :], [], [("ptile", t)], ("ptile", t))

        if samp:
            for r in range(4):
                dma("sp", sttile[0:120, :], st_conf[r * 120:(r + 1) * 120, :], [], [("sig", 0)], "sttile")
                for k in range(4):
                    b = pbp.get()
                    tr(pb[b][:, 0:120], sttile[0:120, k * 128:(k + 1) * 128], ident_f[0:120, 0:120], [("sig", 0), "ident_f"], [pk(b)])
                    cp("dve", cbs[:, k, r * 4:(r + 1) * 4, 0:30], pb[b][:, 0:120].rearrange("p (s j) -> p s j", j=30), [pk(b)], ["cbs"])
                    pbp.put(b)

        def conv_view(buf3, j):
            if samp:
                return buf3[:, :, j:j + L]
            return buf3[:, j:j + L]

        def outv(ap2):
            if samp:
                return ap2.rearrange("p (s l) -> p s l", l=L)
            return ap2

        cb_k = "cbs" if samp else "cbp"
        for half in range(2):
            s = wload([(w_in[:, half * 256:(half + 1) * 256], 8, 0, 256),
                       (w_in[:, 512 + half * 256:512 + (half + 1) * 256], 8, 256, 256)])
            for a in range(2):
                k = half * 2 + a
                bg = pbp.get(); bv = pbp.get()
                for kc in range(8):
                    mm(pb[bg][:, 0:NT], wr[s][:, kc, 256 + a * 128:256 + (a + 1) * 128], hT[:, kc, 0:NT], kc == 0, kc == 7,
                       [("wr", s), "hT"], [pk(bg)])
                for kc in range(8):
                    mm(pb[bv][:, 0:NT], wr[s][:, kc, a * 128:(a + 1) * 128], hT[:, kc, 0:NT], kc == 0, kc == 7,
                       [("wr", s), "hT"], [pk(bv)])
                sg = sig[k % 2]; sgk = ("sig", k % 2)
                act(sg[:, 0:NT], pb[bg][:, 0:NT], AF.Sigmoid, [pk(bg)], [sgk])
                pbp.put(bg)
                if samp:
                    tt("dve", cbs[:, k, :, 30:38], outv(pb[bv][:, 0:NT]), outv(sg[:, 0:NT]), ALU.mult, [pk(bv), sgk], [cb_k])
                    tt("dve", gluf_s[:, k, :], pb[bv][:, 0:NT], sg[:, 0:NT], ALU.mult, [pk(bv), sgk], ["gluf_s"])
                else:
                    tt("dve", cbp[:, k, 30:30 + L], pb[bv][:, 0:NT], sg[:, 0:NT], ALU.mult, [pk(bv), sgk], [cb_k])
                    if last_p:
                        tt("dve", gluf_p[:, k, :], pb[bv][:, NT - 32:NT], sg[:, NT - 32:NT], ALU.mult, [pk(bv), sgk], ["gluf_p"])
                pbp.put(bv)
            wrp.put(s)

        pconv = []

        def qconv(c, qi, d):
            qk_ = ("qb", qi)
            b2 = pbp.get()
            for j in range(DK):
                src = conv_view(qbs[qi][:] if samp else qbp[qi][:], j)
                mm(outv(pb[b2][:, 0:NT]), dgq[d][:, j, :], src, j == 0, j == DK - 1, [("dgq", d), (qk_, "h"), (qk_, "d")], [pk(b2)])
            act(qkvT[:, c, 0:NT], pb[b2][:, 0:NT], AF.Silu, [pk(b2)], [("qkvT", c)])
            pbp.put(b2)
            if not samp:
                cp("pool", qhist[:, c, :], qbp[qi][:, L:L + 3], [(qk_, "d")], ["qhist"])

        for wg in range(4):
            s = wload()
            for cc in range(4):
                c = wg * 4 + cc
                qi = c % 2
                qk_ = ("qb", qi)
                d = c % 2
                if wg < 3:
                    if samp:
                        cp("pool", qbs[qi][:, :, 0:3], qhist_s[:, c, :, :], ["qhist_s"], [(qk_, "h")])
                    else:
                        cp("pool", qbp[qi][:, 0:3], qhist[:, c, :], ["qhist"], [(qk_, "h")])
                    tt("pool", dgq[d][:], ident_b[:].unsqueeze(1).to_broadcast([128, 4, 128]),
                       dnw3[:, :, c].unsqueeze(2).to_broadcast([128, 4, 128]), ALU.mult, ["ident_b", "dnwT"], [("dgq", d)])
                b = pbp.get()
                for kc in range(8):
                    mm(pb[b][:, 0:NT], wr[s][:, kc, cc * 128:(cc + 1) * 128], hT[:, kc, 0:NT], kc == 0, kc == 7,
                       [("wr", s), "hT"], [pk(b)])
                if pconv:
                    pconv.pop(0)()
                if wg == 3:
                    act(szT[:, cc, 0:NT], pb[b][:, 0:NT], AF.Silu, [pk(b)], ["szT"])
                    pbp.put(b)
                    continue
                if samp:
                    cp(evac_eng(), qbs[qi][:, :, 3:11], outv(pb[b][:, 0:NT]), [pk(b)], [(qk_, "d")])
                    cp("dve", qkvf_s[:, c, :, :], outv(pb[b][:, 0:NT])[:, :, 5:8], [pk(b)], ["qkvf_s"])
                else:
                    cp(evac_eng(), qbp[qi][:, 3:3 + L], pb[b][:, 0:NT], [pk(b)], [(qk_, "d")])
                    if last_p:
                        cp("dve", qkvf_p[:, c, :], pb[b][:, NT - 4:NT], [pk(b)], ["qkvf_p"])
                pbp.put(b)
                pconv.append(lambda c=c, qi=qi, d=d: qconv(c, qi, d))
            wrp.put(s)
        while pconv:
            pconv.pop(0)()
        for t in range(ntile):
            b = pbp.get()
            for kc in range(8):
                mm(pb[b][:, 0:8], hT[:, kc, t * 128:(t + 1) * 128], wtail[:, kc, :], kc == 0, kc == 7, ["hT", "wtail"], [pk(b)])
            cp("dve", ba[:, t, :], pb[b][:, 0:8], [pk(b)], [("ba", t)])
            pbp.put(b)

        n4 = ntile * 4
        bav = ba[:, 0:ntile, :]
        baks = [("ba", t) for t in range(ntile)]
        cfv = lambda kind: cf[:, kind, 0:n4]
        cf3 = lambda kind: cf[:, kind, 0:n4].rearrange("p (t h) -> p t h", h=4)
        act(cf3(0), bav[:, :, 0:4], AF.Sigmoid, baks, ["cf"])
        tt("dve", cf3(7), bav[:, :, 4:8], dtb_bc[:].unsqueeze(1).to_broadcast([128, ntile, 4]), ALU.add, baks + ["dtb_bc"], ["cf"])
        act(cfv(7), cfv(7), AF.Exp, ["cf"], ["cf"])
        ts("dve", cfv(7), cfv(7), 1.0, None, ALU.add, None, ["cf"], ["cf"])
        act(cfv(7), cfv(7), AF.Ln, ["cf"], ["cf"])
        tt("dve", cf3(1), cf3(7), negA[:].unsqueeze(1).to_broadcast([128, ntile, 4]), ALU.mult, ["cf", "negA"], ["cf"])
        b = pbp.get()
        for t in range(ntile):
            mm(pb[b][:, t * 4:(t + 1) * 4], Um[v][:], cf[:, 1, t * 4:(t + 1) * 4], True, True, ["U%d" % v, "cf"], [pk(b)])
        mm(pb[b][:, 32:32 + n4], Bsum[v][:], cfv(1), True, True, ["Bsum1", "ones_f", "cf"], [pk(b)])
        act(cfv(2), pb[b][:, 0:n4], AF.Exp, [pk(b)], ["cf"])
        cp("dve", cfv(8), pb[b][:, 32:32 + n4], [pk(b)], ["cf"])
        tt("dve", cfv(7), cfv(8), pb[b][:, 0:n4], ALU.subtract, ["cf", pk(b)], ["cf"])
        pbp.put(b)
        act(cfv(3), cfv(7), AF.Exp, ["cf"], ["cf"])
        act(cfv(6), cfv(8), AF.Exp, ["cf"], ["cf"])
        tt("dve", cfv(4), cfv(0), cfv(2), ALU.mult, ["cf"], ["cf"])
        ts("dve", cfv(5), cfv(2), 128.0 ** -0.5, None, ALU.mult, None, ["cf"], ["cf"])

        def dg_build(k, g):
            j0 = g * 8
            nj = 8 if g < 3 else 7
            d = g % 2
            tt("pool", dgc[d][:, 0:nj, :], ident_b[:].unsqueeze(1).to_broadcast([128, nj, 128]),
               cw3[:, j0:j0 + nj, k].unsqueeze(2).to_broadcast([128, nj, 128]), ALU.mult, ["ident_b", "cwT"], [("dgc", d)])

        def dg_mm(k, g, b):
            j0 = g * 8
            nj = 8 if g < 3 else 7
            d = g % 2
            for jj in range(nj):
                j = j0 + jj
                src = conv_view(cbs[:, k] if samp else cbp[:, k], j)
                mm(outv(pb[b][:, 0:NT]), dgc[d][:, jj, :], src, j == 0, j == CK - 1, [("dgc", d), cb_k], [pk(b)])

        def l2_a(t, bA):
            tc_ = slice(t * 128, (t + 1) * 128)
            sqb = hbs[0]
            for c in range(8):
                tr(pbf(bA)[:, c * 128:(c + 1) * 128], qkvT[:, c, tc_], ident_b[:], [("qkvT", cx) for cx in range(8)] + ["ident_b"], [pk(bA)])
            act(sqb[:], pbf(bA), AF.Square, [pk(bA)], [("hb", 0)])
            P.op("dve", lambda e, sqb=sqb: e.tensor_reduce(out=coef[:, 0:8], in_=sqb[:].rearrange("p (c d) -> p c d", d=128), axis=AX.X, op=ALU.add),
                 reads=[("hb", 0)], writes=["coef"])
            ts("dve", coef[:, 0:8], coef[:, 0:8], EPS, None, ALU.add, None, ["coef"], ["coef"])

        def l2_b(t, bA):
            tmn = hbs[1]
            pw(coef[:, 8:16], coef[:, 0:8], 8, ["coef"], ["coef"])
            tt("dve", tmn[:].rearrange("p (c d) -> p c d", d=128), pbf(bA).rearrange("p (c d) -> p c d", d=128),
               coef[:, 8:16].unsqueeze(2).to_broadcast([128, 8, 128]), ALU.mult, [pk(bA), "coef"], [("hb", 1)])

        def l2_c(t):
            tc_ = slice(t * 128, (t + 1) * 128)
            tmn = hbs[1]
            bB = pbp.get()
            for c in range(8):
                tr(pbf(bB)[:, c * 128:(c + 1) * 128], tmn[:, c * 128:(c + 1) * 128], ident_b[:], [("hb", 1), "ident_b"], [pk(bB)])
            cp("act", qkvT[:, 0:8, tc_], pbf(bB).rearrange("p (c d) -> p c d", d=128), [pk(bB)], [("qkvT", cx) for cx in range(8)])
            pbp.put(bB)

        for k in range(4):
            do_l2 = k < ntile
            if do_l2:
                bA = pbp.get()
                l2_a(k, bA)
            b = pbp.get()
            dg_build(k, 0)
            dg_build(k, 1)
            dg_mm(k, 0, b)
            dg_build(k, 2)
            dg_mm(k, 1, b)
            dg_build(k, 3)
            if do_l2:
                l2_b(k, bA)
                pbp.put(bA)
            dg_mm(k, 2, b)
            dg_mm(k, 3, b)
            act(convf[:, k, 0:NT], pb[b][:, 0:NT], AF.Identity, [pk(b), "colsT"], ["convf"], bias=cwbT[:, k:k + 1])
            pbp.put(b)
            if do_l2:
                l2_c(k)
        if not samp:
            cp("pool", cbp[:, :, 0:30], cbp[:, :, L:L + 30], ["cbp"], ["cbp"])

        bm = pbp.get(); bq = pbp.get()
        for k in range(4):
            mm(pb[bm][:, 0:NT], odiv_f[:], convf[:, k, 0:NT], k == 0, k == 3, ["odiv_f", "convf"], [pk(bm)])
        for k in range(4):
            sq = sqt[k % 2]; sqk = ("sqt", k % 2)
            act(sq[:, 0:NT], convf[:, k, 0:NT], AF.Square, ["convf"], [sqk])
            mm(pb[bq][:, 0:NT], odiv_f[:], sq[:, 0:NT], k == 0, k == 3, ["odiv_f", sqk], [pk(bq)])
        mean = sqt[0]; rst = sqt[1]
        cp("act", mean[:, 0:NT], pb[bm][:, 0:NT], [pk(bm)], [("sqt", 0)])
        act(sig[0][:, 0:NT], pb[bm][:, 0:NT], AF.Square, [pk(bm)], [("sig", 0)])
        tt("dve", sig[1][:, 0:NT], pb[bq][:, 0:NT], sig[0][:, 0:NT], ALU.subtract, [pk(bq), ("sig", 0)], [("sig", 1)])
        pbp.put(bm); pbp.put(bq)
        ts("dve", sig[1][:, 0:NT], sig[1][:, 0:NT], EPS, None, ALU.add, None, [("sig", 1)], [("sig", 1)])
        act(sig[0][:, 0:NT], sig[1][:, 0:NT], AF.Ln, [("sig", 1)], [("sig", 0)])
        act(rst[:, 0:NT], sig[0][:, 0:NT], AF.Exp, [("sig", 0)], [("sqt", 1)], scale=-0.5)
        tt("dve", convf[:, :, 0:NT], convf[:, :, 0:NT], mean[:, 0:NT].unsqueeze(1).to_broadcast([128, 4, NT]), ALU.subtract,
           ["convf", ("sqt", 0)], ["convf"])
        tt("dve", convf[:, :, 0:NT], convf[:, :, 0:NT], rst[:, 0:NT].unsqueeze(1).to_broadcast([128, 4, NT]), ALU.mult,
           ["convf", ("sqt", 1)], ["convf"])
        for k in range(4):
            act(mixT[:, k, 0:NT], convf[:, k, 0:NT], AF.Silu, ["convf", "colsT"], [("mixT", "c")], bias=lnbT[:, k:k + 1], scale=lngT[:, k:k + 1])

        otail = []
        sF = []
        fdone = [0]

        def f_half(t, half):
            if not sF:
                sF.append(wload()); sF.append(wload())
            b = pbp.get()
            for kc in range(8):
                mm(pb[b][:], mixT[:, kc, t * 128:(t + 1) * 128], wr[sF[half]][:, kc, :], kc == 0, kc == 7,
                   [("mixT", "c"), ("mixT", t), ("wr", sF[half])], [pk(b)])
            tt("dve", xt[:, t, half * 512:(half + 1) * 512], xt[:, t, half * 512:(half + 1) * 512], pb[b][:], ALU.add,
               [("xt", t), pk(b)], [("xt", t)])
            pbp.put(b)

        def f_tile(t):
            if not sF:
                sF.append(wload()); sF.append(wload())
            for half in range(2):
                b = pbp.get()
                for kc in range(8):
                    mm(pb[b][:], mixT[:, kc, t * 128:(t + 1) * 128], wr[sF[half]][:, kc, :], kc == 0, kc == 7,
                       [("mixT", "c"), ("mixT", t), ("wr", sF[half])], [pk(b)])
                tt("dve", xt[:, t, half * 512:(half + 1) * 512], xt[:, t, half * 512:(half + 1) * 512], pb[b][:], ALU.add,
                   [("xt", t), pk(b)], [("xt", t)])
                pbp.put(b)
            norm_T(t, gffnT, hT, "hT")
            fdone[0] = t + 1

        for t in range(ntile):
            tc_ = slice(t * 128, (t + 1) * 128)
            t4 = slice(t * 4, (t + 1) * 4)
            beta = cf[:, 0, t4]; G = cf[:, 1, t4]
            bc4 = lambda ap: ap.unsqueeze(2).to_broadcast([128, 4, 128])

            def build_gU(tt_):
                Gx = cf[:, 1, tt_ * 4:(tt_ + 1) * 4]
                tt("dve", gU[:], Um[v][:].unsqueeze(1).to_broadcast([128, 4, 128]), Gx.unsqueeze(2).to_broadcast([128, 4, 128]), ALU.mult,
                   ["U%d" % v, "cf"], ["gU"])
            if t == 0:
                build_gU(0)

            bT = pbp.get()
            for h in range(4):
                mm(pb[bT][:, h * 128:(h + 1) * 128], Ms[v][:], gU[:, h, :], True, False, ["Ms%d" % v, "gU"], [pk(bT)])
                mm(pb[bT][:, h * 128:(h + 1) * 128], ident_b[:], NEGT[v][:], False, True, ["ident_b", "NEGT%d" % v], [pk(bT)])
            act(ETm[:].rearrange("p h i -> p (h i)"), pb[bT][:], AF.Exp, [pk(bT)], ["ETm"])
            pbp.put(bT)
            bE = pbp.get()
            for h in range(4):
                mm(pb[bE][:, h * 128:(h + 1) * 128], gU[:, h, :], Ms[v][:], True, False, ["gU", "Ms%d" % v], [pk(bE)])
                mm(pb[bE][:, h * 128:(h + 1) * 128], ident_b[:], NEGS[v][:], False, True, ["ident_b", "NEGS%d" % v], [pk(bE)])
            act(Em[:].rearrange("p h i -> p (h i)"), pb[bE][:], AF.Exp, [pk(bE)], ["Em"])
            pbp.put(bE)
            tt("dve", Em[:], Em[:], bc4(cf[:, 0, t4]), ALU.mult, ["Em", "cf"], ["Em"])
            if t + 1 < ntile:
                build_gU(t + 1)
            bK = pbp.get(); bQ = pbp.get()
            for h in range(4):
                mm(pb[bK][:, h * 128:(h + 1) * 128], qkvT[:, 4 + h, tc_], qkvT[:, 4 + h, tc_], True, True, [("qkvT", cx) for cx in range(12)], [pk(bK)])
            for h in range(4):
                mm(pb[bQ][:, h * 128:(h + 1) * 128], qkvT[:, 4 + h, tc_], qkvT[:, h, tc_], True, True, [("qkvT", cx) for cx in range(12)], [pk(bQ)])
            b = pbp.get(); b2 = pbp.get()
            for c in range(8):
                tr(pbf(b)[:, c * 128:(c + 1) * 128], qkvT[:, c, tc_], ident_b[:], [("qkvT", cx) for cx in range(12)] + ["ident_b"], [pk(b)])
            for c in range(4):
                tr(pbf(b2)[:, c * 128:(c + 1) * 128], qkvT[:, 8 + c, tc_], ident_b[:], [("qkvT", cx) for cx in range(12)] + ["ident_b"], [pk(b2)])
            cp("act", tmqk[:], pbf(b).rearrange("p (c d) -> p c d", d=128), [pk(b)], ["tmqk"])
            cp("dve", tmv[:], pbf(b2)[:, 0:512].rearrange("p (c d) -> p c d", d=128), [pk(b2)], ["tmv"])
            pbp.put(b); pbp.put(b2)
            tt("pool", vb[:], tmv[:], bc4(cf[:, 0, t4]), ALU.mult, ["tmv", "cf"], ["vb"])
            tt("pool", kbe[:], tmqk[:, 4:8, :], bc4(cf[:, 4, t4]), ALU.mult, ["tmqk", "cf"], ["kbe"])
            tt("pool", kdec[:], tmqk[:, 4:8, :], bc4(cf[:, 3, t4]), ALU.mult, ["tmqk", "cf"], ["kdec"])
            tt("pool", qdec[:], tmqk[:, 0:4, :], bc4(cf[:, 5, t4]), ALU.mult, ["tmqk", "cf"], ["qdec"])

            tt("dve", A_bf[:].rearrange("p h i -> p (h i)"), pb[bK][:], Em[:].rearrange("p h i -> p (h i)"), ALU.mult, [pk(bK), "Em"], ["A_bf"])
            stt("dve", qkT[:].rearrange("p h i -> p (h i)"), pb[bQ][:], 128.0 ** -0.5, ETm[:].rearrange("p h i -> p (h i)"), ALU.mult, ALU.mult,
                [pk(bQ), "ETm"], ["qkT"])
            pbp.put(bK); pbp.put(bQ)
            b = pbp.get()
            for h in range(4):
                tr(pbf(b)[:, h * 128:(h + 1) * 128], A_bf[:, h, :], ident_b[:], ["A_bf", "ident_b"], [pk(b)])
            pv = pbf(b)[:, 0:512].rearrange("p (h i) -> p h i", i=128)
            cp("act", Pm[0][:], pv, [pk(b)], [("Pm", 0)])
            act(Rm[0][:], pv, AF.Identity, [pk(b)], [("Rm", 0)], scale=-1.0)
            tt("pool", Rm[0][:], Rm[0][:], ident_b[:].unsqueeze(1).to_broadcast([128, 4, 128]), ALU.add, ["ident_b", ("Rm", 0)], [("Rm", 0)])
            pbp.put(b)
            cur = 0
            rc = 0
            PTcur = A_bf
            PTkey = "A_bf"

            def r_update(PTl, PTlk, rc):
                bR = pbp.get()
                for h in range(4):
                    mm(pb[bR][:, h * 128:(h + 1) * 128], ident_b[:], Rm[rc][:, h, :], True, False, ["ident_b", ("Rm", rc)], [pk(bR)])
                    mm(pb[bR][:, h * 128:(h + 1) * 128], PTl[:, h, :], Rm[rc][:, h, :], False, True, [PTlk, ("Rm", rc)], [pk(bR)])
                cp(evac_eng(), Rm[1 - rc][:].rearrange("p h i -> p (h i)"), pb[bR][:], [pk(bR)], [("Rm", 1 - rc)])
                pbp.put(bR)

            PTbuf = [sttl for sttl in PTm] + [PT3]
            for lvl in range(1, 7):
                nxt = 1 - cur
                pti = lvl % 3
                bPT = pbp.get()
                for h in range(4):
                    mm(pb[bPT][:, h * 128:(h + 1) * 128], Pm[cur][:, h, :], PTcur[:, h, :], True, True, [("Pm", cur), PTkey], [pk(bPT)])
                if lvl < 6:
                    bP = pbp.get()
                    for h in range(4):
                        mm(pb[bP][:, h * 128:(h + 1) * 128], PTcur[:, h, :], Pm[cur][:, h, :], True, True, [("Pm", cur), PTkey], [pk(bP)])
                cp("act", PTbuf[pti][:].rearrange("p h i -> p (h i)"), pb[bPT][:], [pk(bPT)], [("PTm", pti)])
                pbp.put(bPT)
                if lvl < 6:
                    cp("dve", Pm[nxt][:].rearrange("p h i -> p (h i)"), pb[bP][:], [pk(bP)], [("Pm", nxt)])
                    pbp.put(bP)
                if lvl > 1:
                    r_update(PTcur, PTkey, rc)
                    rc = 1 - rc
                if t >= 1 and not samp:
                    if lvl == 1 and otail:
                        otail.pop(0)()
                    if lvl == 3:
                        f_half(t - 1, 0)
                    if lvl == 4:
                        f_half(t - 1, 1)
                        norm_front(t - 1)
                    if lvl == 6:
                        norm_back(t - 1, gffnT, hT, "hT")
                        fdone[0] = t
                PTcur = PTbuf[pti]; PTkey = ("PTm", pti)
                cur = nxt
            r_update(PTcur, PTkey, rc)
            rc = 1 - rc
            cur = rc
            if otail:
                otail.pop(0)()
            TT = Rm[cur]; TTk = ("Rm", cur)
            bu = pbp.get(); bw = pbp.get(); bq = pbp.get()
            for h in range(4):
                mm(pb[bu][:, h * 128:(h + 1) * 128], TT[:, h, :], vb[:, h, :], True, True, [TTk, "vb"], [pk(bu)])
            for h in range(4):
                mm(pb[bw][:, h * 128:(h + 1) * 128], kbe[:, h, :], TT[:, h, :], True, True, [TTk, "kbe"], [pk(bw)])
            for h in range(4):
                tr(pbf(bq)[:, h * 128:(h + 1) * 128], qdec[:, h, :], ident_b[:], ["qdec", "ident_b"], [pk(bq)])
            cp("act", u_sb[:].rearrange("p h i -> p (h i)"), pb[bu][:], [pk(bu)], ["u_sb"])
            cp("dve", wT[:].rearrange("p h i -> p (h i)"), pb[bw][:], [pk(bw)], ["wT"])
            cp("act", qdT[:].rearrange("p h i -> p (h i)"), pbf(bq)[:, 0:512], [pk(bq)], ["qdT"])
            pbp.put(bu); pbp.put(bw); pbp.put(bq)

            bo = pbp.get()
            if not samp:
                bv_ = pbp.get()
                for h in range(4):
                    mm(pb[bv_][:, h * 128:(h + 1) * 128], wT[:, h, :], S_bf[:, h, :], True, True, ["wT", "S_bf"], [pk(bv_)])
                tt("dve", vnew[:].rearrange("p h i -> p (h i)"), u_sb[:].rearrange("p h i -> p (h i)"), pb[bv_][:], ALU.subtract,
                   ["u_sb", pk(bv_)], ["vnew"])
                pbp.put(bv_)
                for h in range(4):
                    mm(pb[bo][:, h * 128:(h + 1) * 128], qdT[:, h, :], S_bf[:, h, :], True, False, ["qdT", "S_bf"], [pk(bo)])
                    mm(pb[bo][:, h * 128:(h + 1) * 128], qkT[:, h, :], vnew[:, h, :], False, True, ["qkT", "vnew"], [pk(bo)])
                bs = pbp.get()
                for h in range(4):
                    mm(pb[bs][:, h * 128:(h + 1) * 128], kdec[:, h, :], vnew[:, h, :], True, True, ["kdec", "vnew"], [pk(bs)])
                tt("dve", S_f[:], S_f[:], cf[:, 6, t4].unsqueeze(2).to_broadcast([128, 4, 128]), ALU.mult, ["S_f", "cf"], ["S_f"])
                tt("dve", S_f[:].rearrange("p h i -> p (h i)"), S_f[:].rearrange("p h i -> p (h i)"), pb[bs][:], ALU.add, ["S_f", pk(bs)], ["S_f"])
                pbp.put(bs)
                cp("act", S_bf[:], S_f[:], ["S_f"], ["S_bf"])
            else:
                P.barrier()
                memset("pool", wTx[:], 0.0, ["wTx"])
                memset("pool", qdTx[:], 0.0, ["qdTx"])
                tt("dve", GB[:], Bsel[:].unsqueeze(2).to_broadcast([128, 16, 4]), G.unsqueeze(1).to_broadcast([128, 16, 4]), ALU.mult,
                   ["Bsel", "cf"], ["GB"])
                b = pbp.get()
                mm(pb[b][:, 0:64], ones_f[:], GB[:].rearrange("p s h -> p (s h)"), True, True, ["ones_f", "GB"], [pk(b)])
                act(egts[:].rearrange("p s h -> p (s h)"), pb[b][:, 0:64], AF.Exp, [pk(b)], ["egts"])
                pbp.put(b)
                wx3 = lambda tns: tns[:].rearrange("p (s r) -> p s r", r=136)[:, 0:16, 0:8]
                for h in range(4):
                    dma("sp", S0f[:], st_S[:, h, :, :].rearrange("s k v -> k s v"), [], ["S0f"], "S0f")
                    cp("dve", S0b[:, 0:8, :], S0f[:, 0:8, :], ["S0f"], ["S0b"])
                    cp("act", S0b[:, 8:16, :], S0f[:, 8:16, :], ["S0f"], ["S0b"])
                    cp("pool", wx3(wTx), wT[:, h, :].rearrange("p (s c) -> p s c", c=8), ["wT"], ["wTx"])
                    cp("pool", wx3(qdTx), qdT[:, h, :].rearrange("p (s c) -> p s c", c=8), ["qdT"], ["qdTx"])
                    bv_ = pbp.get()
                    for s_ in range(16):
                        mm(pb[bv_][:, 0:128], wTx[:, s_ * 128:(s_ + 1) * 128], S0b[:, s_, :], s_ == 0, s_ == 15, ["wTx", "S0b"], [pk(bv_)])
                    tt("dve", vnew[:, h, :], u_sb[:, h, :], pb[bv_][:, 0:128], ALU.subtract, ["u_sb", pk(bv_)], ["vnew"])
                    pbp.put(bv_)
                    for s_ in range(16):
                        mm(pb[bo][:, h * 128:(h + 1) * 128], qdTx[:, s_ * 128:(s_ + 1) * 128], S0b[:, s_, :], s_ == 0, False, ["qdTx", "S0b"], [pk(bo)])
                    mm(pb[bo][:, h * 128:(h + 1) * 128], qkT[:, h, :], vnew[:, h, :], False, True, ["qkT", "vnew"], [pk(bo)])
                    for g4 in range(4):
                        if g4 % 2 == 0:
                            tt("pool", kdx[:], kdec[:, h, :].unsqueeze(1).to_broadcast([128, 8, 128]),
                               Bsel[:, g4 * 4:g4 * 4 + 8].unsqueeze(2).to_broadcast([128, 8, 128]), ALU.mult, ["kdec", "Bsel"], ["kdx"])
                        bs = pbp.get()
                        for s4 in range(4):
                            s_ = g4 * 4 + s4
                            mm(pb[bs][:, s4 * 128:(s4 + 1) * 128], kdx[:, s_ % 8, :], vnew[:, h, :], True, True, ["kdx", "vnew"], [pk(bs)])
                        for s4 in range(4):
                            s_ = g4 * 4 + s4
                            stt("dve", Sout[:, s_, :], S0f[:, s_, :], egts[:, s_, h:h + 1], pb[bs][:, s4 * 128:(s4 + 1) * 128], ALU.mult, ALU.add,
                                ["S0f", "egts", pk(bs)], ["Sout"])
                        pbp.put(bs)
                    dma("sp", o_S_s[:, h, :, :].rearrange("s k v -> k s v"), Sout[:], ["Sout"], [], "Sout")
                P.barrier()
            cp("act", o_sb[:].rearrange("p h i -> p (h i)"), pb[bo][:], [pk(bo)], ["o_sb"])
            pbp.put(bo)
            tt("pool", u_sb[:], o_sb[:], o_sb[:], ALU.mult, ["o_sb", "u_sb"], ["u_sb"])
            P.op("dve", lambda e: e.tensor_reduce(out=coef[:, 28:32], in_=u_sb[:], axis=AX.X, op=ALU.add), reads=["u_sb"], writes=["coef"])
            ts("dve", coef[:, 28:32], coef[:, 28:32], 1.0 / 128, EPS, ALU.mult, ALU.add, ["coef"], ["coef"])
            pw(coef[:, 32:36], coef[:, 28:32], 4, ["coef"], ["coef"])
            tt("dve", on_bf[:], o_sb[:], coef[:, 32:36].unsqueeze(2).to_broadcast([128, 4, 128]), ALU.mult, ["o_sb", "coef"], ["on_bf"])

            def _otail(tc_=tc_):
                b = pbp.get()
                for h in range(4):
                    tr(pbf(b)[:, h * 128:(h + 1) * 128], on_bf[:, h, :], ident_b[:], ["on_bf", "ident_b"], [pk(b)])
                stt("dve", mixT[:, 4:8, tc_], pbf(b)[:, 0:512].rearrange("p (h i) -> p h i", i=128), dngT, szT[:, :, tc_], ALU.mult, ALU.mult,
                    [pk(b), "colsT", "szT"], [("mixT", tc_.start // 128)])
                pbp.put(b)
            otail.append(_otail)
        if last_p:
            dma("sp", o_S_p.rearrange("h k v -> k h v"), S_f[:], ["S_f"], [], "S_f_out")

        while otail:
            otail.pop(0)()
        while fdone[0] < ntile:
            f_tile(fdone[0])
        for s_ in sF:
            wrp.put(s_)

        pfc = []

        def fconv(i, ui, chunks):
            uk = ("ub", ui)
            for a in range(2):
                bg = pbp.get(); bv_ = pbp.get()
                for (bb, q4) in ((bg, a), (bv_, 2 + a)):
                    for j in range(FK - 1):
                        src = conv_view(ubs[ui][:, q4] if samp else ubp[ui][:, q4], j)
                        mm(outv(pb[bb][:, 0:NT]), dgf[ui][:, q4, j, :], src, j == 0, j == FK - 2,
                           [("dgf", ui), (uk, "h"), (uk, q4)], [pk(bb)])
                sg = sig[a]; sgk = ("sig", a)
                tv = sqt[0]; tvk = ("sqt", 0)
                cg = chunks[a]; cv = chunks[2 + a]
                curg = conv_view(ubs[ui][:, a] if samp else ubp[ui][:, a], FK - 1)
                curv = conv_view(ubs[ui][:, 2 + a] if samp else ubp[ui][:, 2 + a], FK - 1)
                stt("dve", outv(sg[:, 0:NT]), curg, fw3[:, FK - 1, cg:cg + 1], outv(pb[bg][:, 0:NT]), ALU.mult, ALU.add,
                    [(uk, a), "fwT", pk(bg)], [sgk])
                pbp.put(bg)
                stt("dve", outv(tv[:, 0:NT]), curv, fw3[:, FK - 1, cv:cv + 1], outv(pb[bv_][:, 0:NT]), ALU.mult, ALU.add,
                    [(uk, 2 + a), "fwT", pk(bv_)], [tvk])
                pbp.put(bv_)
                act(sg[:, 0:NT], sg[:, 0:NT], AF.Silu, [sgk, "colsT"], [sgk], bias=fbT[:, cg:cg + 1])
                stt("dve", actT[:, 2 * i + a, 0:NT], tv[:, 0:NT], fbT[:, cv:cv + 1], sg[:, 0:NT], ALU.add, ALU.mult,
                    [tvk, "colsT", sgk], ["actT"])
            if not samp:
                for q4 in range(4):
                    cp("pool", uhist[:, chunks[q4], :], ubp[ui][:, q4, L:L + 2], [(uk, q4)], ["uhist"])

        for i in range(11):
            ui = i % 2
            uk = ("ub", ui)
            chunks = [2 * i, 2 * i + 1, 22 + 2 * i, 22 + 2 * i + 1]
            if samp:
                for q4 in range(4):
                    cp("pool", ubs[ui][:, q4, :, 0:2], uhist_s[:, chunks[q4], :, :], ["uhist_s"], [(uk, "h")])
            else:
                for q4 in range(4):
                    cp("pool", ubp[ui][:, q4, 0:2], uhist[:, chunks[q4], :], ["uhist"], [(uk, "h")])
            s = wload()
            if not samp and (hgroups or hpend):
                hist_step()
            for q4 in range(4):
                tt("pool", dgf[ui][:, q4, 0:2, :], ident_b[:].unsqueeze(1).to_broadcast([128, 2, 128]),
                   fw3[:, 0:2, chunks[q4]].unsqueeze(2).to_broadcast([128, 2, 128]), ALU.mult, ["ident_b", "fwT"], [("dgf", ui)])
            for q4 in range(4):
                b = pbp.get()
                for kc in range(8):
                    mm(pb[b][:, 0:NT], wr[s][:, kc, q4 * 128:(q4 + 1) * 128], hT[:, kc, 0:NT], kc == 0, kc == 7, [("wr", s), "hT"], [pk(b)])
                if samp:
                    cp(evac_eng(), ubs[ui][:, q4, :, 2:10], outv(pb[b][:, 0:NT]), [pk(b)], [(uk, q4)])
                    cp("dve", upf_s[:, chunks[q4], :].rearrange("p (s j) -> p s j", j=2), outv(pb[b][:, 0:NT])[:, :, 6:8], [pk(b)], ["upf_s"])
                else:
                    cp(evac_eng(), ubp[ui][:, q4, 2:2 + L], pb[b][:, 0:NT], [pk(b)], [(uk, q4)])
                    if last_p:
                        cp("dve", upf_p[:, chunks[q4], :], pb[b][:, NT - 4:NT], [pk(b)], ["upf_p"])
                pbp.put(b)
                if q4 == 1 and pfc:
                    pfc.pop(0)()
            wrp.put(s)
            pfc.append(lambda i=i, ui=ui, chunks=chunks: fconv(i, ui, chunks))
        while hpend:
            _hg_tr(*hpend.pop(0))
        while pfc:
            pfc.pop(0)()

        for half in range(2):
            accs = [pbp.get() for _ in range(ntile)]
            for gi, (k0, nk) in enumerate(((0, 8), (8, 8), (16, 6))):
                s = wload([(w_down[k0 * 128:(k0 + nk) * 128, half * 512:(half + 1) * 512], nk, 0, 512)])
                for t in range(ntile):
                    for kk in range(nk):
                        kc = k0 + kk
                        mm(pb[accs[t]][:], actT[:, kc, t * 128:(t + 1) * 128], wr[s][:, kk, :], kc == 0, kc == 21,
                           ["actT", ("wr", s)], [pk(accs[t])])
                wrp.put(s)
            for t in range(ntile):
                tt("dve", xt[:, t, half * 512:(half + 1) * 512], xt[:, t, half * 512:(half + 1) * 512], pb[accs[t]][:], ALU.add,
                   [("xt", t), pk(accs[t])], [("xt", t)])
                pbp.put(accs[t])

        for t in range(ntile):
            norm_T(t, None, hT, "hT", do_norm=False)
            ptile = ptile4[:, t, :]
            b = pbp.get()
            for k2 in range(2):
                tr(pbf(b)[:, k2 * 128:(k2 + 1) * 128], ptile[:, k2 * 128:(k2 + 1) * 128], ident_b[:], [("ptile", t), "ident_b"], [pk(b)])
            cp("act", pT[:, :, t * 128:(t + 1) * 128], pbf(b)[:, 0:256].rearrange("p (k c) -> p k c", c=128), [pk(b)], ["pT"])
            pbp.put(b)
        sJ = [wload(), wload()]
        ydst = y_s if samp else y_p[blk * 512:(blk + 1) * 512, :]
        for t in range(ntile):
            if blk < 3 and t >= 2:
                norm_front(t - 2)
            for half in range(2):
                bg = pbp.get(); be = pbp.get()
                for kc in range(8):
                    mm(pb[bg][:], hT[:, kc, t * 128:(t + 1) * 128], wr[sJ[half]][:, kc, :], kc == 0, kc == 7,
                       [("hTj", t), ("wr", sJ[half])], [pk(bg)])
                for k2 in range(2):
                    mm(pb[be][:], pT[:, k2, t * 128:(t + 1) * 128], wple_sb[:, k2, half * 512:(half + 1) * 512], k2 == 0, k2 == 1,
                       ["pT", "wple_sb"], [pk(be)])
                sg = sig[half]; sgk = ("sig", half)
                act(sg[:], pb[bg][:], AF.Sigmoid, [pk(bg)], [sgk])
                pbp.put(bg)
                tt("dve", sg[:], sg[:], pb[be][:], ALU.mult, [sgk, pk(be)], [sgk])
                pbp.put(be)
                tt("dve", xt[:, t, half * 512:(half + 1) * 512], xt[:, t, half * 512:(half + 1) * 512], sg[:], ALU.add,
                   [("xt", t), sgk], [("xt", t)])
            if blk < 3 and t >= 2:
                norm_back(t - 2, gmixT, hT, "hT")
            xk = ("xt", t)
            sm = small[:, 24 + t * 4:24 + t * 4 + 4]; smk = ("smallf", t)
            act(junk[:], xt[:, t, :], AF.Square, [xk], [("sqt", 0), smk], accum=sm[:, 0:1])
            ts("dve", sm[:, 1:2], sm[:, 0:1], 1.0 / D, EPS, ALU.mult, ALU.add, [smk], [smk])
            pw(sm[:, 3:4], sm[:, 1:2], 1, [smk], [smk])
            stt("dve", xt[:, t, :], xt[:, t, :], sm[:, 3:4], gfin_bc[:], ALU.mult, ALU.mult, [xk, smk, "gfin_bc"], [xk])
            dma("sp", ydst[t * 128:(t + 1) * 128, :], xt[:, t, :], [xk], [], ("yo", t))
            if blk < 3:
                nsrc = x_p[(blk + 1) * 512:(blk + 2) * 512, :]
                dma("sp", xt[:, t, :], nsrc[t * 128:(t + 1) * 128, :], [], [("xt", t)], ("xt", t))
        if blk < 3:
            norm_T(ntile - 2, gmixT, hT, "hT")
            norm_T(ntile - 1, gmixT, hT, "hT")
            pre_a[0] = True
        for s_ in sJ:
            wrp.put(s_)

        oc = [0]

        def tr_out(src_ap_fn, nch, ncol, dst_rows_fn):
            for g0 in range(0, nch, 4):
                b = pbp.get()
                for c in range(g0, min(g0 + 4, nch)):
                    tr(pb[b][0:ncol, (c - g0) * 128:(c - g0 + 1) * 128], src_ap_fn(c), ident_f[:], ["ident_f", "stsrc"], [pk(b)])
                n = min(4, nch - g0) * 128
                oi = oc[0] % 2; oc[0] += 1
                ob = sig[oi]; okey = ("sig", oi); dk = ("orow", oi)
                cp("dve", ob[0:ncol, 0:n], pb[b][0:ncol, 0:n], [pk(b)], [okey])
                pbp.put(b)
                dst_rows_fn(g0, n, ob, okey, dk)

        if last_p:
            P.op("dve", lambda e: e.tensor_copy(out=small[:, 63:64], in_=small[:, 63:64]), reads=["gluf_p", "qkvf_p", "upf_p"], writes=["stsrc"])
            tr_out(lambda c: gluf_p[:, c, :], 4, 32, lambda g0, n, ob, okey, dk: dma("sp", o_conf_p[:, :], ob[2:32, 0:512], [okey], [], dk))
            tr_out(lambda c: qkvf_p[:, c, :], 12, 4,
                   lambda g0, n, ob, okey, dk: dma("sp", o_dn_p[:, g0 * 128:g0 * 128 + n], ob[1:4, 0:n], [okey], [], dk))
            tr_out(lambda c: upf_p[:, c, :], 44, 4,
                   lambda g0, n, ob, okey, dk: dma("sp", o_ffn_p[:, g0 * 128:g0 * 128 + n], ob[2:4, 0:n], [okey], [], dk))
        if samp:
            P.op("dve", lambda e: e.tensor_copy(out=small[:, 63:64], in_=small[:, 63:64]), reads=["gluf_s", "qkvf_s", "upf_s"], writes=["stsrc"])

            def conf_s_out(g0, n, ob, okey, dk):
                for s_ in range(16):
                    dma("sp", o_conf_s[s_, 22:30, :], ob[s_ * 8:(s_ + 1) * 8, 0:512], [okey], [], dk)
            tr_out(lambda c: gluf_s[:, c, :], 4, 128, conf_s_out)
            dma("sp", o_conf_s[:, 0:22, :], st_conf.rearrange("(s j) c -> s j c", j=30)[:, 8:30, :], [], [], "d2d")

            tr_out(lambda c: qkvf_s[:, c, :, :].rearrange("p s j -> p (s j)"), 12, 48,
                   lambda g0, n, ob, okey, dk: dma("sp", o_dn_s.rearrange("s j c -> (s j) c")[:, g0 * 128:g0 * 128 + n], ob[0:48, 0:n], [okey], [], dk))
            tr_out(lambda c: upf_s[:, c, :], 44, 32,
                   lambda g0, n, ob, okey, dk: dma("sp", o_ffn_s[:, g0 * 128:g0 * 128 + n], ob[0:32, 0:n], [okey], [], dk))

    for blk in range(5):
        if blk == 4:
            while hgroups or hpend:
                hist_step()
            P.barrier()
        import os
        if os.environ.get("MK_DBG"):
            print("block", blk, "starts at op", len(P.ops))
        do_block(blk)

    P.emit()
    st.close()
    return nc, P.stats


_CACHE = {}


def kernel(x_prompt, x_sample, p_prompt, p_sample, state_conf_buf, state_dn_conv_buf, state_dn_S, state_ffn_buf,
           norm_mix_g, w_in, conf_dw_w, conf_dw_b, conf_ln_g, conf_ln_b, dn_conv_w, dn_a_log, dn_dt_bias,
           dn_norm_g, w_out, norm_ffn_g, w_up, ffn_conv_w, ffn_conv_b, w_down, w_ple, w_ple_gate, norm_final_g):
    f = lambda a: np.ascontiguousarray(np.asarray(a, dtype=np.float32))
    if "nc" not in _CACHE:
        _CACHE["nc"] = build_program()[0]
    nc = _CACHE["nc"]
    shared = dict(
        g_mix=f(norm_mix_g).reshape(8, 128), w_in=f(w_in)[0], cw_w=f(conf_dw_w).reshape(CK * 4, 128),
        cw_b=f(conf_dw_b).reshape(4, 128), ln_g=f(conf_ln_g).reshape(4, 128), ln_b=f(conf_ln_b).reshape(4, 128),
        dn_w=f(dn_conv_w).reshape(DK * 12, 128), a_log=f(dn_a_log).reshape(1, 4), dt_b=f(dn_dt_bias).reshape(1, 4),
        dn_g=f(dn_norm_g).reshape(1, 128), w_out=f(w_out)[0], g_ffn=f(norm_ffn_g).reshape(8, 128), w_up=f(w_up)[0],
        f_w=f(ffn_conv_w).reshape(FK * 44, 128), f_b=f(ffn_conv_b).reshape(44, 128), w_down=f(w_down)[0],
        w_ple=f(w_ple)[0], w_pg=f(w_ple_gate)[0], g_fin=f(norm_final_g).reshape(1, D))
    xp = f(x_prompt); xs = f(x_sample); pp = f(p_prompt)[0]; ps_ = f(p_sample)[0]
    sc = f(state_conf_buf)[0]; sd = f(state_dn_conv_buf)[0]; sS = f(state_dn_S)[0]; sf = f(state_ffn_buf)[0]
    in_maps = []
    for c in range(NCORE):
        sl = slice(c * DEC_PER, (c + 1) * DEC_PER)
        m = dict(shared)
        m.update(x_p=xp[c], x_s=xs[sl].reshape(128, D), p_p=pp[c], p_s=ps_[sl].reshape(128, 256),
                 st_conf=sc[sl].reshape(DEC_PER * 30, CW), st_dn=sd[sl].reshape(DEC_PER * 3, 3 * DNW),
                 st_S=np.ascontiguousarray(sS[sl]), st_ffn=sf[sl].reshape(DEC_PER * 2, 2 * DFF))
        in_maps.append(m)
    res = run_bass_kernel_spmd(nc, in_maps, core_ids=list(range(NCORE)))
    R = res.results
    cat = lambda k: np.stack([np.asarray(R[c][k], dtype=np.float32) for c in range(NCORE)], axis=0)
    y_prompt = cat("y_p")
    y_sample = cat("y_s").reshape(128, DEC_SEQ, D)
    pc = cat("o_conf_p")[None]
    pd = cat("o_dn_p")[None]
    ps = cat("o_S_p")[None]
    pf = cat("o_ffn_p")[None]
    sc_o = cat("o_conf_s").reshape(1, 128, 30, CW)
    sd_o = cat("o_dn_s").reshape(1, 128, 3, 3 * DNW)
    ss_o = cat("o_S_s").reshape(1, 128, 4, 128, 128)
    sf_o = cat("o_ffn_s").reshape(1, 128, 2, 2 * DFF)
    return (y_prompt, y_sample, pc, pd, ps, pf, sc_o, sd_o, ss_o, sf_o)
```
